# Optimizing a Trainium2 kernel written in Bass

```python
import jax, jax.numpy as jnp
from jax import lax
import numpy as np

D_MODEL = 2048
BATCH = 4
SEQ = 2048
DEPTH = 2
DEC_BATCH = 128
DEC_SEQ = 8
PAST_LEN = 16384
PAGE_SIZE = 128

W_A = D_MODEL // 4
N_HEADS_A = 4
CONV_K = 31
W_B = D_MODEL // 4
N_HEADS_B = 4
HEAD_B = W_B // N_HEADS_B
CHUNK = 128
W_C = D_MODEL // 2
SSM_GROUP = 16
N_GROUPS_C = W_C // SSM_GROUP
SSM_STATE = 64
W_MIX = W_A + W_B + W_C
D_IN = 2 * W_A + 2 * W_B + W_C
D_FF = -(-(8 * D_MODEL) // (3 * 256)) * 256
EPS = 1e-6
LOG_DT_MIN = -6.907755
LOG_DT_MAX = -2.302585

kernel_name = "hymba_conv_gmlp_s5_decode_step"


def rmsnorm(x, g):
    x32 = x.astype(jnp.float32)
    y = x32 * lax.rsqrt(jnp.mean(x32 * x32, axis=-1, keepdims=True) + EPS)
    return (y * g.astype(jnp.float32)).astype(x.dtype)


def layernorm_heads(x, g, b, n_heads):
    n, t, c = x.shape
    x32 = x.astype(jnp.float32).reshape(n, t, n_heads, c // n_heads)
    mu = jnp.mean(x32, axis=-1, keepdims=True)
    xc = x32 - mu
    var = jnp.mean(xc * xc, axis=-1, keepdims=True)
    y = (xc * lax.rsqrt(var + EPS)).reshape(n, t, c)
    return (y * g.astype(jnp.float32) + b.astype(jnp.float32)).astype(x.dtype)


def conformer_conv(a_val, a_gate, conv_prev, conv_w, conv_b, ln_g, ln_b):
    z = a_val * jax.nn.sigmoid(a_gate)
    zp = jnp.concatenate([conv_prev.astype(z.dtype), z], axis=1)
    y = lax.conv_general_dilated(zp, conv_w[:, None, :].astype(z.dtype), window_strides=(1,), padding="VALID",
                                 dimension_numbers=("NWC", "WIO", "NWC"), feature_group_count=W_A)
    y = layernorm_heads(y + conv_b, ln_g, ln_b, N_HEADS_A)
    return jax.nn.silu(y), zp[:, -(CONV_K - 1):]


def chunk_spatial_gate(u, v, ln_g, ln_b, w_s, b_s):
    n, t, _ = v.shape
    vn = layernorm_heads(v, ln_g, ln_b, 1)
    n_chunks = -(-t // CHUNK)
    pad = n_chunks * CHUNK - t
    vp = jnp.pad(vn, ((0, 0), (0, pad), (0, 0))).reshape(n, n_chunks, CHUNK, N_HEADS_B, HEAD_B)
    mask = jnp.tril(jnp.ones((CHUNK, CHUNK), dtype=bool))
    w = jnp.where(mask, w_s, jnp.zeros_like(w_s))
    mixed = jnp.einsum("hts,ncshd->ncthd", w, vp) + jnp.transpose(b_s)[:, :, None]
    mixed = mixed.reshape(n, n_chunks * CHUNK, W_B)[:, :t]
    return u * mixed, vn


def s5_branch(u, h0_re, h0_im, a_re, a_im, log_dt, b_re, b_im, c_re, c_im, d_skip, w_glu, b_glu):
    f32 = jnp.float32
    n, t, _ = u.shape
    ug = u.astype(f32).reshape(n, t, N_GROUPS_C, SSM_GROUP)
    dt = jnp.exp(log_dt.astype(f32))[:, None]
    ar = a_re.astype(f32)
    ai = a_im.astype(f32)
    mag = jnp.exp(dt * ar)
    abar_re = mag * jnp.cos(dt * ai)
    abar_im = mag * jnp.sin(dt * ai)
    p_re = abar_re - 1.0
    p_im = abar_im
    den = ar * ar + ai * ai
    k_re = (p_re * ar + p_im * ai) / den
    k_im = (p_im * ar - p_re * ai) / den
    br = b_re.astype(f32)
    bi = b_im.astype(f32)
    bbar_re = k_re[..., None] * br - k_im[..., None] * bi
    bbar_im = k_re[..., None] * bi + k_im[..., None] * br
    bu_re = jnp.einsum("ntgc,gpc->ntgp", ug, bbar_re)
    bu_im = jnp.einsum("ntgc,gpc->ntgp", ug, bbar_im)
    h0r = h0_re.astype(f32)
    h0i = h0_im.astype(f32)
    bu_re = bu_re.at[:, 0].add(abar_re * h0r - abar_im * h0i)
    bu_im = bu_im.at[:, 0].add(abar_re * h0i + abar_im * h0r)
    a_re_t = jnp.broadcast_to(abar_re, bu_re.shape)
    a_im_t = jnp.broadcast_to(abar_im, bu_im.shape)

    def combine(e1, e2):
        a1r, a1i, b1r, b1i = e1
        a2r, a2i, b2r, b2i = e2
        return (a1r * a2r - a1i * a2i,
                a1r * a2i + a1i * a2r,
                a2r * b1r - a2i * b1i + b2r,
                a2r * b1i + a2i * b1r + b2i)

    _, _, hr, hi = lax.associative_scan(combine, (a_re_t, a_im_t, bu_re, bu_im), axis=1)
    y = (jnp.einsum("ntgp,gcp->ntgc", hr, c_re.astype(f32))
         - jnp.einsum("ntgp,gcp->ntgc", hi, c_im.astype(f32))
         + d_skip.astype(f32) * ug)
    z = jax.nn.gelu(y)
    gl = jnp.einsum("ntgc,gck->ntgk", z, w_glu.astype(f32)) + b_glu.astype(f32)
    out = gl[..., :SSM_GROUP] * jax.nn.sigmoid(gl[..., SSM_GROUP:])
    return (out.reshape(n, t, W_C).astype(u.dtype),
            hr[:, -1].astype(h0_re.dtype),
            hi[:, -1].astype(h0_im.dtype))


def hybrid_layer(x, conv_prev, h0_re, h0_im, prm):
    h = rmsnorm(x, prm["g_mix"])
    proj = h @ prm["w_in"]
    a_val, a_gate, b_u, b_v, c_u = jnp.split(
        proj, [W_A, 2 * W_A, 2 * W_A + W_B, 2 * W_A + 2 * W_B], axis=-1)
    out_a, conv_new = conformer_conv(a_val, a_gate, conv_prev, prm["conv_w"], prm["conv_b"],
                                     prm["ln_a_g"], prm["ln_a_b"])
    out_b, v_rows = chunk_spatial_gate(b_u, b_v, prm["ln_v_g"], prm["ln_v_b"], prm["w_s"], prm["b_s"])
    out_c, hr, hi = s5_branch(c_u, h0_re, h0_im, prm["a_re"], prm["a_im"], prm["log_dt"],
                              prm["b_re"], prm["b_im"], prm["c_re"], prm["c_im"],
                              prm["d_skip"], prm["w_glu"], prm["b_glu"])
    x = x + jnp.concatenate([out_a, out_b, out_c], axis=-1) @ prm["w_out"]
    h2 = rmsnorm(x, prm["g_ffn"])
    gate, up = jnp.split(h2 @ prm["w_gu"], 2, axis=-1)
    x = x + (jax.nn.silu(gate) * up) @ prm["w_down"]
    return x, conv_new, hr, hi, v_rows


def setup_inputs(seed: int = 0) -> dict:
    key = jax.random.key(seed)
    ks = jax.random.split(key, 32)
    nrm = jax.random.normal
    f32 = jnp.float32
    inp = {}
    inp["x_prompt"] = nrm(ks[0], (BATCH, SEQ, D_MODEL), f32)
    inp["x_sample"] = nrm(ks[1], (DEC_BATCH, DEC_SEQ, D_MODEL), f32)
    inp["state_conv"] = 0.5 * nrm(ks[2], (DEPTH, DEC_BATCH, CONV_K - 1, W_A), f32)
    inp["state_ssm_re"] = 0.05 * nrm(ks[3], (DEPTH, DEC_BATCH, N_GROUPS_C, SSM_STATE), f32)
    inp["state_ssm_im"] = 0.05 * nrm(ks[4], (DEPTH, DEC_BATCH, N_GROUPS_C, SSM_STATE), f32)
    inp["g_mix"] = 1.0 + 0.02 * nrm(ks[5], (DEPTH, D_MODEL), f32)
    inp["w_in"] = nrm(ks[6], (DEPTH, D_MODEL, D_IN), f32) * D_MODEL ** -0.5
    inp["conv_w"] = nrm(ks[7], (DEPTH, CONV_K, W_A), f32) * CONV_K ** -0.5
    inp["conv_b"] = 0.02 * nrm(ks[8], (DEPTH, W_A), f32)
    inp["ln_a_g"] = 1.0 + 0.02 * nrm(ks[9], (DEPTH, W_A), f32)
    inp["ln_a_b"] = 0.02 * nrm(ks[10], (DEPTH, W_A), f32)
    inp["ln_v_g"] = 1.0 + 0.02 * nrm(ks[11], (DEPTH, W_B), f32)
    inp["ln_v_b"] = 0.02 * nrm(ks[12], (DEPTH, W_B), f32)
    inp["w_s"] = nrm(ks[13], (DEPTH, N_HEADS_B, CHUNK, CHUNK), f32) * CHUNK ** -0.5
    inp["b_s"] = 1.0 + 0.02 * nrm(ks[14], (DEPTH, N_HEADS_B, CHUNK), f32)
    inp["a_re"] = -0.5 + 0.01 * nrm(ks[15], (DEPTH, N_GROUPS_C, SSM_STATE), f32)
    inp["a_im"] = jnp.tile(jnp.pi * jnp.arange(SSM_STATE, dtype=f32), (DEPTH, N_GROUPS_C, 1))
    inp["log_dt"] = jax.random.uniform(ks[16], (DEPTH, N_GROUPS_C), f32, LOG_DT_MIN, LOG_DT_MAX)
    inp["b_re"] = nrm(ks[17], (DEPTH, N_GROUPS_C, SSM_STATE, SSM_GROUP), f32) * (2 * SSM_GROUP) ** -0.5
    inp["b_im"] = nrm(ks[18], (DEPTH, N_GROUPS_C, SSM_STATE, SSM_GROUP), f32) * (2 * SSM_GROUP) ** -0.5
    inp["c_re"] = nrm(ks[19], (DEPTH, N_GROUPS_C, SSM_GROUP, SSM_STATE), f32) * SSM_STATE ** -0.5
    inp["c_im"] = nrm(ks[20], (DEPTH, N_GROUPS_C, SSM_GROUP, SSM_STATE), f32) * SSM_STATE ** -0.5
    inp["d_skip"] = nrm(ks[21], (DEPTH, N_GROUPS_C, SSM_GROUP), f32)
    inp["w_glu"] = nrm(ks[22], (DEPTH, N_GROUPS_C, SSM_GROUP, 2 * SSM_GROUP), f32) * SSM_GROUP ** -0.5
    inp["b_glu"] = 0.02 * nrm(ks[23], (DEPTH, N_GROUPS_C, 2 * SSM_GROUP), f32)
    inp["w_out"] = nrm(ks[24], (DEPTH, W_MIX, D_MODEL), f32) * W_MIX ** -0.5
    inp["g_ffn"] = 1.0 + 0.02 * nrm(ks[25], (DEPTH, D_MODEL), f32)
    inp["w_gu"] = nrm(ks[26], (DEPTH, D_MODEL, 2 * D_FF), f32) * D_MODEL ** -0.5
    inp["w_down"] = nrm(ks[27], (DEPTH, D_FF, D_MODEL), f32) * D_FF ** -0.5
    inp["g_final"] = 1.0 + 0.02 * nrm(ks[28], (D_MODEL,), f32)
    return inp


def reference(x_prompt, x_sample, state_conv, state_ssm_re, state_ssm_im, g_mix, w_in, conv_w, conv_b,
              ln_a_g, ln_a_b, ln_v_g, ln_v_b, w_s, b_s, a_re, a_im, log_dt, b_re, b_im, c_re, c_im,
              d_skip, w_glu, b_glu, w_out, g_ffn, w_gu, w_down, g_final):
    xp = x_prompt
    xs = x_sample
    n_p = x_prompt.shape[0]
    conv_p, ssr_p, ssi_p = [], [], []
    conv_s, ssr_s, ssi_s, v_s = [], [], [], []
    for l in range(DEPTH):
        prm = {"g_mix": g_mix[l], "w_in": w_in[l], "conv_w": conv_w[l], "conv_b": conv_b[l],
               "ln_a_g": ln_a_g[l], "ln_a_b": ln_a_b[l], "ln_v_g": ln_v_g[l], "ln_v_b": ln_v_b[l],
               "w_s": w_s[l], "b_s": b_s[l], "a_re": a_re[l], "a_im": a_im[l], "log_dt": log_dt[l],
               "b_re": b_re[l], "b_im": b_im[l], "c_re": c_re[l], "c_im": c_im[l], "d_skip": d_skip[l],
               "w_glu": w_glu[l], "b_glu": b_glu[l], "w_out": w_out[l], "g_ffn": g_ffn[l],
               "w_gu": w_gu[l], "w_down": w_down[l]}
        zero_conv = jnp.zeros((n_p, CONV_K - 1, W_A), xp.dtype)
        zero_ssm = jnp.zeros((n_p, N_GROUPS_C, SSM_STATE), state_ssm_re.dtype)
        xp, cp, hrp, hip, _ = hybrid_layer(xp, zero_conv, zero_ssm, zero_ssm, prm)
        conv_p.append(cp)
        ssr_p.append(hrp)
        ssi_p.append(hip)
        xs, cs, hrs, his, vs = hybrid_layer(xs, state_conv[l], state_ssm_re[l], state_ssm_im[l], prm)
        conv_s.append(cs)
        ssr_s.append(hrs)
        ssi_s.append(his)
        v_s.append(vs)
    y_prompt = rmsnorm(xp, g_final)
    y_sample = rmsnorm(xs, g_final)
    new_conv_prompt = jnp.stack(conv_p)
    new_ssm_re_prompt = jnp.stack(ssr_p)
    new_ssm_im_prompt = jnp.stack(ssi_p)
    new_conv_sample = jnp.stack(conv_s)
    new_ssm_re_sample = jnp.stack(ssr_s)
    new_ssm_im_sample = jnp.stack(ssi_s)
    new_chunk_v_sample = jnp.stack(v_s)
    return (y_prompt, y_sample, new_conv_prompt, new_ssm_re_prompt, new_ssm_im_prompt,
            new_conv_sample, new_ssm_re_sample, new_ssm_im_sample, new_chunk_v_sample)
```

```python
import contextlib
import numpy as np
import concourse.bass as bass
import concourse.mybir as mybir
from concourse.bass_utils import run_bass_kernel_spmd

F32 = mybir.dt.float32
BF16 = mybir.dt.bfloat16
ALU = mybir.AluOpType
AF = mybir.ActivationFunctionType

D = 2048; NT = 1152; LT = 1024; NB = 9; DEPTH = 2; DFF = 5632
EPS = 1e-6
ENGS = ["tensor", "vector", "scalar", "gpsimd", "sync"]
NCORES = 8


class Ins:
    __slots__ = ("eng", "fn", "dma", "deps", "needed", "sig", "idx", "cc", "ses")

    def __init__(self, eng, fn, dma):
        self.eng = eng; self.fn = fn; self.dma = dma; self.deps = []; self.needed = False; self.sig = None; self.cc = False; self.ses = True


class Sched:
    def __init__(self):
        self.instrs = {e: [] for e in ENGS}
        self.last_w = {}
        self.readers = {}
        self.n = 0

    def add(self, eng, fn, reads=(), writes=(), dma=False, cc=False, ses=True):
        ins = Ins(eng, fn, dma or cc)
        ins.cc = cc; ins.ses = ses
        ins.idx = self.n; self.n += 1
        deps = {}
        for k in reads:
            w = self.last_w.get(k)
            if w is not None: deps[w.idx] = w
        for k in writes:
            w = self.last_w.get(k)
            if w is not None: deps[w.idx] = w
            for r in self.readers.get(k, ()): deps[r.idx] = r
        ins.deps = list(deps.values())
        for d in ins.deps: d.needed = True
        for k in reads: self.readers.setdefault(k, []).append(ins)
        for k in writes:
            self.last_w[k] = ins; self.readers[k] = []
        self.instrs[eng].append(ins)
        return ins

    def emit(self, nc, block, stack):
        SEG = 3000
        NDS = 6
        esems = {}
        for e in ENGS:
            ncomp = sum(1 for i in self.instrs[e] if (not i.dma) and i.needed)
            esems[e] = [stack.enter_context(nc.semaphore(f"e_{e}_{k}")) for k in range(ncomp // SEG + 1)]
            cnt = 0
            for i in self.instrs[e]:
                if (not i.dma) and i.needed:
                    i.sig = (esems[e][cnt // SEG], cnt % SEG + 1); cnt += 1
        dsems = {e: [stack.enter_context(nc.semaphore(f"d_{e}_{k}")) for k in range(NDS)] for e in ENGS
                 if any(i.dma for i in self.instrs[e])}
        for e in ENGS:
            if e not in dsems: continue
            k = 0; tot = [0] * NDS
            ccsem = None; cccnt = 0
            for i in self.instrs[e]:
                if i.cc:
                    if ccsem is None: ccsem = stack.enter_context(nc.semaphore(f"cc_{e}"))
                    cccnt += 1
                    i.sig = (ccsem, cccnt, -1)
                elif i.dma:
                    s = k % NDS; k += 1; tot[s] += 16
                    i.sig = (dsems[e][s], tot[s], s)

        def stream(e):
            def body(eng):
                waited = {}
                prev_on = {}

                def wait(sem, val):
                    key = id(sem)
                    if waited.get(key, 0) >= val: return
                    eng.wait_ge(sem, val); waited[key] = val
                for ins in self.instrs[e]:
                    for d in ins.deps:
                        if d.dma:
                            wait(d.sig[0], d.sig[1])
                        elif d.eng != e or (e in SAME_ENG_SYNC and ins.ses):
                            wait(d.sig[0], d.sig[1])
                    if ins.cc:
                        ins.fn(eng).then_inc(ins.sig[0], 1)
                    elif ins.dma:
                        s = ins.sig[2]
                        if s in prev_on: wait(ins.sig[0], prev_on[s])
                        prev_on[s] = ins.sig[1]
                        ins.fn(eng).then_inc(ins.sig[0], 16)
                    else:
                        r = ins.fn(eng)
                        if ins.needed: r.then_inc(ins.sig[0], 1)
                for s, v in prev_on.items():
                    wait(dsems[e][s], v)
            return body
        for e in ENGS:
            if self.instrs[e]:
                getattr(block, e)(stream(e))


DEBUG_TAPS = False
USE_CC = True
SAME_ENG_SYNC = ("vector", "scalar", "gpsimd", "sync")
TAP_NAMES = []


def build_program():
    nc = bass.Bass("TRN2", target_bir_lowering=False)
    S = Sched()
    st = contextlib.ExitStack()

    def din(name, shape, dt=F32):
        return nc.dram_tensor(name, list(shape), dt, kind="ExternalInput").ap()

    def dout(name, shape):
        return nc.dram_tensor(name, list(shape), F32, kind="ExternalOutput").ap()

    def sb(name, shape, dt=F32):
        return st.enter_context(nc.sbuf_tensor(name, list(shape), dt))

    xT_d = din("xT", [128, 16 * NT])
    zs0_d = din("zs0", [DEPTH, 128, 4 * 16 * 30])
    ss0_d = din("ss0", [DEPTH, 128, 64 * 16])
    w_in_d = din("w_in", [DEPTH, 24, 128, 2048])
    w_out_d = din("w_out", [DEPTH, 16, 128, 2048])
    w_gu_d = din("w_gu", [DEPTH, 44, 128, 4096])
    w_dn_d = din("w_dn", [DEPTH, 64, 128, 1408])
    pv_d = din("pvec", [DEPTH, 128, 256])
    gfin_d = din("gfin", [128, 16])
    lnv_d = din("lnv", [DEPTH, 128, 1024])
    ws_d = din("ws", [DEPTH, 128, 1024])
    bs_d = din("bs", [DEPTH, 128, 1024])
    mask_d = din("mask", [128, 512])
    ssmR_d = din("ssmR", [DEPTH, 3, 128, 4096])
    bpad_d = din("bpad", [DEPTH, 2, 128, 4096])
    cpad_d = din("cpad", [DEPTH, 128, 64 * 128])
    ssmS_d = din("ssmS", [DEPTH, 128, 768])
    wg_d = din("wg", [DEPTH, 128, 2048])

    cmask_d = din("cmask", [128, 2])
    bin_d = [nc.dram_tensor(f"xch_in{l}", [128, 184], F32).ap() for l in range(DEPTH)]
    bout_d = [nc.dram_tensor(f"xch_out{l}", [128, 184], F32).ap() for l in range(DEPTH)]
    yT_o = dout("yT", [128, 16 * NT])
    convp_o = dout("conv_p", [DEPTH, 128, 120])
    ssmp_o = dout("ssm_p", [DEPTH, 128, 64])
    convs_o = dout("conv_s", [DEPTH, 128, 4 * 16 * 30])
    ssms_o = dout("ssm_s", [DEPTH, 128, 64 * 16])
    vs_o = dout("v_s", [DEPTH, 128, 512])

    xT = sb("xT_sb", [128, 16, NT])
    RSZ = 54016
    R = sb("R", [128, RSZ], BF16)
    pv = sb("pv", [128, 256])
    gfin = sb("gfin_sb", [128, 16])
    ones_b = sb("ones_b", [128, 128], BF16)
    ones_f = sb("ones_f", [128, 128])
    mask = sb("mask_sb", [128, 512], BF16)
    small = sb("small", [128, 64])
    epsc = sb("epsc", [128, 1])
    cmask = sb("cmask_sb", [128, 2])
    sendb = sb("sendb", [128, 184])
    recvb = sb("recvb", [128, 184])
    nint = sb("nint", [128, 256], mybir.dt.int32)
    rstd = sb("rstd", [128, 128])
    win = [sb(f"win{i}", [128, 16, 128], BF16) for i in range(2)]

    off = [0]

    def carve(n_bf16, dt=BF16, shape=None):
        a = off[0]; off[0] += n_bf16
        v = R[:, a:a + n_bf16]
        if dt == F32:
            v = v.bitcast(F32)
        return v

    hT_all = carve(16 * NT).rearrange("p (k t) -> p k t", k=16)
    act = carve(11 * NT).rearrange("p (k t) -> p k t", k=11)
    wgu = [carve(4096).rearrange("p (k c) -> p k c", k=16) for _ in range(2)]
    wdn = [carve(1408).rearrange("p (k c) -> p k c", k=11) for _ in range(2)]
    sgt = carve(1024, F32)
    ffn_end = off[0]
    off[0] = 0
    hblk = carve(2048).rearrange("p (k t) -> p k t", k=16)
    mix = carve(2048).rearrange("p (k t) -> p k t", k=16)
    cuT = carve(1024).rearrange("p (k t) -> p k t", k=8)
    vnb = carve(512)
    wsb = carve(1024)
    wgb = carve(2048).rearrange("p (k c) -> p k c", k=16)
    bbp = carve(8192).rearrange("p (k c) -> p k c", k=64)
    cpd = carve(8192).rearrange("p (k c) -> p k c", k=64)
    g_off = off[0]
    bu = carve(2 * 1024, F32).rearrange("p (k t) -> p k t", k=64)
    hbf = carve(1024).rearrange("p (k t) -> p k t", k=64)
    zg = carve(1024).rearrange("p (k t) -> p k t", k=8)
    G = [R[:, g_off + 512 * i:g_off + 512 * (i + 1)].bitcast(F32) for i in range(8)]
    zbuf = carve(2 * 4 * 158, F32).rearrange("p (k t) -> p k t", k=4)
    zs = carve(2 * 4 * 16 * 38, F32).rearrange("p (k s t) -> p k s t", k=4, s=16)
    ub = carve(2 * 512, F32).rearrange("p (k t) -> p k t", k=4)
    vnf = carve(2 * 512, F32)
    junk = carve(2 * 512, F32)
    tA = carve(2 * 512, F32)
    tB = carve(2 * 1024, F32)
    wsf = tB
    tC = carve(2 * 512, F32)
    lnv = carve(2 * 1024, F32)
    bsb = carve(2 * 512, F32)
    Sst = carve(2 * 64, F32)
    Ssm = carve(2 * 1024, F32).rearrange("p (k s) -> p k s", k=64)
    Sso = Ssm
    AR = carve(2 * 64, F32)
    AI2 = carve(2 * 64, F32)
    AR4 = carve(2 * 256, F32).rearrange("p (k s) -> p k s", k=64)
    AI4 = carve(2 * 256, F32).rearrange("p (k s) -> p k s", k=64)
    rt1 = carve(2 * 256, F32)
    rt2 = carve(2 * 256, F32)
    sS = carve(2 * 768, F32)
    if off[0] < ffn_end: off[0] = ffn_end
    yst_raw = carve(2 * 1024)
    yst = yst_raw.bitcast(F32).rearrange("p (k t) -> p k t", k=8)
    xsq = yst_raw.rearrange("p (k t) -> p k t", k=16)
    mixer_end = off[0]
    print("R usage (bf16 elems)", ffn_end, mixer_end)
    assert max(ffn_end, mixer_end) <= RSZ, (ffn_end, mixer_end)

    ps = [st.enter_context(nc.psum_tensor(f"ps{i}", [128, 512], F32)) for i in range(8)]

    PV_GMIX, PV_GFFN, PV_CW, PV_CB, PV_LAG, PV_LAB, PV_DSK, PV_BV, PV_BG = 0, 16, 32, 156, 160, 164, 168, 176, 184

    V, A, T_, G_, SY = "vector", "scalar", "tensor", "gpsimd", "sync"

    def dma(eng, out, in_, r, w):
        S.add(eng, lambda e: e.dma_start(out=out, in_=in_), reads=r, writes=w, dma=True)

    def tt(out, a, b, op, r, w, ses=True):
        S.add(V, lambda e: e.tensor_tensor(out=out, in0=a, in1=b, op=op), reads=r, writes=w, ses=ses)

    def ts(out, a, s1, s2, op0, op1, r, w):
        if op1 is None:
            S.add(V, lambda e: e.tensor_scalar(out=out, in0=a, scalar1=s1, scalar2=None, op0=op0), reads=r, writes=w)
        else:
            S.add(V, lambda e: e.tensor_scalar(out=out, in0=a, scalar1=s1, scalar2=s2, op0=op0, op1=op1), reads=r, writes=w)

    def stt(out, a, s, b, op0, op1, r, w, ses=True):
        S.add(V, lambda e: e.scalar_tensor_tensor(out=out, in0=a, scalar=s, in1=b, op0=op0, op1=op1), reads=r, writes=w, ses=ses)

    def actf(out, in_, func, r, w, bias=None, scale=None, accum=None):
        kw = {}
        if bias is not None: kw["bias"] = bias
        if scale is not None: kw["scale"] = scale
        if accum is not None: kw["accum_out"] = accum
        S.add(A, lambda e: e.activation(out=out, in_=in_, func=func, **kw), reads=r, writes=w)

    def mm(out, lhsT, rhs, start, stop, r, w):
        S.add(T_, lambda e: e.matmul(out, lhsT, rhs, start=start, stop=stop), reads=r, writes=w)

    def rsqrt(out, in_, scale, r, w):
        npart = out.shape[0]
        actf(out, in_, AF.Sqrt, list(r) + ["epsc"], w, bias=epsc[0:npart, :], scale=scale)
        S.add(V, lambda e: e.reciprocal(out=out, in_=out), reads=w, writes=w)

    def tap(name, ap, r):
        if not DEBUG_TAPS: return
        shp = list(ap.shape)
        d = nc.dram_tensor("tap_" + name, shp, ap.dtype, kind="ExternalOutput").ap()
        TAP_NAMES.append("tap_" + name)
        dma(SY, d, ap, r, [])

    def vcopy(out, in_, r, w):
        S.add(V, lambda e: e.tensor_copy(out=out, in_=in_), reads=r, writes=w)

    def vmemset(ap, val, w):
        S.add(V, lambda e: e.memset(ap, val), reads=(), writes=w)

    dma(SY, xT[:].rearrange("p k t -> p (k t)"), xT_d, (), ["xT"])
    dma(SY, gfin[:], gfin_d, (), ["gfin"])
    dma(SY, cmask[:], cmask_d, (), ["cmask"])
    dma(G_, mask[:], mask_d, (), ["mask"])
    vmemset(ones_b[:], 1.0, ["ones_b"])
    vmemset(epsc[:], EPS, ["epsc"])
    vmemset(ones_f[:], 1.0 / 128.0, ["ones_f"])

    psi = [0]

    def next_ps():
        i = psi[0] % 6; psi[0] += 1
        return i

    wsl = [0]

    def load_w(src):
        i = wsl[0] % 2; wsl[0] += 1
        dma(G_, win[i][:].rearrange("p k c -> p (k c)"), src, (), [f"win{i}"])
        return i

    def rmsnorm_block(c0, n, gcol, out_fn, out_keys, xkeys):
        actf(xsq[:, :, :n], xT[:, :, c0:c0 + n], AF.Square, xkeys, ["yst"])
        for kt in range(16):
            mm(ps[7][:, :n], ones_b[:], xsq[:, kt, :n], kt == 0, kt == 15, ["ones_b", "yst"], ["ps7"])
        rsqrt(rstd[:, :n], ps[7][:, :n], 1.0 / D, ["ps7"], ["rstd"])
        for kt in range(16):
            stt(out_fn(kt), xT[:, kt, c0:c0 + n], gcol(kt), rstd[:, :n], ALU.mult, ALU.mult,
                xkeys + ["rstd", "pv", "gfin"], out_keys)

    for l in range(DEPTH):
        dma(SY, pv[:], pv_d[l], (), ["pv"])
        dma(SY, lnv, lnv_d[l], (), ["lnv"])
        dma(SY, wsf, ws_d[l], (), ["wsf"])
        dma(SY, bsb, bs_d[l, :, 0:512], (), ["bsb"])
        dma(SY, sS, ssmS_d[l], (), ["sS"])
        dma(SY, zs[:, :, :, 0:30], zs0_d[l].rearrange("p (k s t) -> p k s t", k=4, s=16), (), ["zs"])
        dma(SY, Ssm[:].rearrange("p k s -> p (k s)"), ss0_d[l], (), ["Ssm"])
        dma(G_, wgb[:].rearrange("p k c -> p (k c)"), wg_d[l], (), ["wgb"])
        dma(G_, cpd[:].rearrange("p k c -> p (k c)"), cpad_d[l], (), ["cpd"])
        tt(wsb[:, 0:512], wsf[:, 0:512], mask[:], ALU.mult, ["wsf", "mask"], ["wsb"])
        tt(wsb[:, 512:1024], wsf[:, 512:1024], mask[:], ALU.mult, ["wsf", "mask"], ["wsb"])

        def abar_gen(are, aim, ldt, n, tmp, out_r, out_i, key_in, key_out, want_k=None):
            dt_, mg, ang, sn, cs = tmp[0], tmp[1], tmp[2], tmp[3], tmp[4]
            actf(dt_, ldt, AF.Exp, key_in, ["g0"])
            tt(mg, dt_, are, ALU.mult, ["g0"] + key_in, ["g1"])
            actf(mg, mg, AF.Exp, ["g1"], ["g1"])
            tt(ang, dt_, aim, ALU.mult, ["g0"] + key_in, ["g2"])
            nf = tB[:, 0:n]; ni = nint[:, 0:n]; mm_ = tC[:, 0:n]
            for (dst, phase, key) in ((sn, 0.0, "g3"), (cs, 0.5 * np.pi, "g4")):
                ts(dst, ang, phase, None, ALU.add, None, ["g2"], [key])
                ts(nf, dst, 1.0 / (2 * np.pi), None, ALU.mult, None, [key], ["tBa"])
                vcopy(ni, nf, ["tBa"], ["tBb"])
                vcopy(nf, ni, ["tBb"], ["tBa"])
                stt(dst, nf, -2 * np.pi, dst, ALU.mult, ALU.add, ["tBa", key], [key])
                ts(mm_, dst, np.pi, -2 * np.pi, ALU.is_gt, ALU.mult, [key], ["tCa"])
                tt(dst, dst, mm_, ALU.add, [key, "tCa"], [key])
                ts(mm_, dst, -np.pi, 2 * np.pi, ALU.is_lt, ALU.mult, [key], ["tCa"])
                tt(dst, dst, mm_, ALU.add, [key, "tCa"], [key])
                actf(dst, dst, AF.Sin, [key], [key])
            tt(out_r, mg, cs, ALU.mult, ["g1", "g4"], key_out)
            tt(out_i, mg, sn, ALU.mult, ["g1", "g3"], key_out)

        gt = [g[:] for g in G]
        abar_gen(sS[:, 0:256], sS[:, 256:512], sS[:, 512:768], 256, gt, tA[:, 0:256], tA[:, 256:512], ["sS"], ["tA"])
        vcopy(AR[:, 0:32], tA[:, 0:32], ["tA"], ["AR"])
        vcopy(AR[:, 32:64], tA[:, 0:32], ["tA"], ["AR"])
        vcopy(AI2[:, 32:64], tA[:, 256:288], ["tA"], ["AI2"])
        ts(AI2[:, 0:32], tA[:, 256:288], -1.0, None, ALU.mult, None, ["tA"], ["AI2"])
        for s4 in range(4):
            vcopy(AR4[:, :, s4], AR[:], ["AR"], ["AR4"])
            vcopy(AI4[:, :, s4], AI2[:], ["AI2"], ["AI4"])

        for ch in range(16):
            c0 = ch * 256
            for j in range(3):
                dma(SY, G[5 + j][:], ssmR_d[l, j, :, c0:c0 + 256], (), [f"gin{j}"])
            are, aim, ldt = G[5][:], G[6][:], G[7][:]
            gt = [g[:] for g in G]
            abr, abi = tA[:, 0:256], tA[:, 256:512]
            abar_gen(are, aim, ldt, 256, gt, abr, abi, ["gin0", "gin1", "gin2"], ["tA"])
            den, kr, ki, t0 = tB[:, 0:256], tB[:, 256:512], tC[:, 0:256], tC[:, 256:512]
            tt(den, are, are, ALU.mult, ["gin0"], ["tBa"])
            tt(t0, aim, aim, ALU.mult, ["gin1"], ["tCb"])
            tt(den, den, t0, ALU.add, ["tBa", "tCb"], ["tBa"])
            S.add(V, lambda e, den=den: e.reciprocal(out=den, in_=den), reads=["tBa"], writes=["tBa"])
            ts(abr, abr, -1.0, None, ALU.add, None, ["tA"], ["tA"])
            tt(kr, abr, are, ALU.mult, ["tA", "gin0"], ["tBb"])
            tt(t0, abi, aim, ALU.mult, ["tA", "gin1"], ["tCb"])
            tt(kr, kr, t0, ALU.add, ["tBb", "tCb"], ["tBb"])
            tt(kr, kr, den, ALU.mult, ["tBb", "tBa"], ["tBb"])
            tt(ki, abi, are, ALU.mult, ["tA", "gin0"], ["tCa"])
            tt(t0, abr, aim, ALU.mult, ["tA", "gin1"], ["tCb"])
            tt(ki, ki, t0, ALU.subtract, ["tCa", "tCb"], ["tCa"])
            tt(ki, ki, den, ALU.mult, ["tCa", "tBa"], ["tCa"])
            dma(SY, G[0][:], bpad_d[l, 0, :, c0:c0 + 256], (), ["g0"])
            dma(SY, G[1][:], bpad_d[l, 1, :, c0:c0 + 256], (), ["g1"])
            bre, bim = G[0][:], G[1][:]
            o_re = bbp[:, ch * 2:(ch + 1) * 2, :].rearrange("p k c -> p (k c)")
            o_im = bbp[:, 32 + ch * 2:32 + (ch + 1) * 2, :].rearrange("p k c -> p (k c)")
            tt(G[2][:], kr, bre, ALU.mult, ["tBb", "g0"], ["g2"])
            tt(G[3][:], ki, bim, ALU.mult, ["tCa", "g1"], ["g3"])
            tt(o_re, G[2][:], G[3][:], ALU.subtract, ["g2", "g3"], ["bbp"])
            tt(G[2][:], kr, bim, ALU.mult, ["tBb", "g1"], ["g2"])
            tt(G[3][:], ki, bre, ALU.mult, ["tCa", "g0"], ["g3"])
            tt(o_im, G[2][:], G[3][:], ALU.add, ["g2", "g3"], ["bbp"])

        def proj(ct, swap=False, out_ap=None):
            wi = load_w(w_in_d[l, ct])
            p = next_ps() if out_ap is None else None
            o = ps[p][:, :128] if out_ap is None else out_ap
            okey = [f"ps{p}"] if out_ap is None else ["ps6"]
            for kt in range(16):
                if swap:
                    mm(o, hblk[:, kt, :], win[wi][:, kt, :], kt == 0, kt == 15, ["hblk", f"win{wi}"], okey)
                else:
                    mm(o, win[wi][:, kt, :], hblk[:, kt, :], kt == 0, kt == 15, ["hblk", f"win{wi}"], okey)
            return p


        def ssm_block(smp, pass1):
            for sub in range(8):
                t0 = sub * 16
                for bank in range(2):
                    for j in range(32):
                        idx = bank * 32 + j
                        q = idx % 32
                        mm(ps[bank][:, j * 16:(j + 1) * 16], bbp[:, idx, :], cuT[:, q // 4, t0:t0 + 16], True, True,
                           ["bbp", "cuT"], [f"ps{bank}"])
                    actf(bu[:, bank * 32:(bank + 1) * 32, :], ps[bank][:].rearrange("p (k t) -> p k t", k=32), AF.Copy,
                         [f"ps{bank}"], ["bu"])
                if not smp:
                    for t in range(16):
                        prev = Sst if t == 0 else bu[:, :, t - 1]
                        pk = ["Sst"] if t == 0 else ["bu"]
                        r1, r2 = rt1[:, 0:64], rt2[:, 0:64]
                        tt(r1, AR, prev, ALU.mult, ["AR"] + pk, ["rt1"], ses=False)
                        tt(r2[:, 0:32], AI2[:, 0:32], prev[:, 32:64], ALU.mult, ["AI2"] + pk, ["rt2"], ses=False)
                        tt(r2[:, 32:64], AI2[:, 32:64], prev[:, 0:32], ALU.mult, ["AI2"] + pk, ["rt2"], ses=False)
                        tt(r1, r1, r2, ALU.add, ["rt1", "rt2"], ["rt1"], ses=False)
                        tt(bu[:, :, t], bu[:, :, t], r1, ALU.add, ["bu", "rt1"], ["bu"], ses=False)
                    vcopy(Sst, bu[:, :, 15], ["bu"], ["Sst"])
                else:
                    bu4 = bu[:].rearrange("p k (s t) -> p k s t", s=2)
                    for t in range(8):
                        prev = Ssm[:, :, sub * 2:(sub + 1) * 2] if t == 0 else bu4[:, :, :, t - 1]
                        pk = ["Ssm"] if t == 0 else ["bu"]
                        r1 = rt1[:, 0:128].rearrange("p (k s) -> p k s", k=64)
                        r2 = rt2[:, 0:128].rearrange("p (k s) -> p k s", k=64)
                        tt(r1, AR4[:, :, 0:2], prev, ALU.mult, ["AR4"] + pk, ["rt1"], ses=False)
                        tt(r2[:, 0:32, :], AI4[:, 0:32, 0:2], prev[:, 32:64, :], ALU.mult, ["AI4"] + pk, ["rt2"], ses=False)
                        tt(r2[:, 32:64, :], AI4[:, 32:64, 0:2], prev[:, 0:32, :], ALU.mult, ["AI4"] + pk, ["rt2"], ses=False)
                        tt(r1, r1, r2, ALU.add, ["rt1", "rt2"], ["rt1"], ses=False)
                        tt(bu4[:, :, :, t], bu4[:, :, :, t], r1, ALU.add, ["bu", "rt1"], ["bu"], ses=False)
                    vcopy(Sso[:, :, sub * 2:(sub + 1) * 2], bu4[:, :, :, 7], ["bu"], ["Sso"])
                if pass1:
                    continue
                actf(hbf[:, 0:32, :], bu[:, 0:32, :], AF.Copy, ["bu"], ["hbf"])
                actf(hbf[:, 32:64, :], bu[:, 32:64, :], AF.Copy, ["bu"], ["hbf"], scale=-1.0)
                for Tt in range(8):
                    n = 0
                    for r_ in range(2):
                        for qq in range(4):
                            idx = r_ * 32 + 4 * Tt + qq
                            mm(ps[4][:, Tt * 16:(Tt + 1) * 16], cpd[:, idx, :], hbf[:, idx, :], n == 0, n == 7,
                               ["cpd", "hbf"], ["ps4"])
                            n += 1
                actf(yst[:, :, t0:t0 + 16], ps[4][:, 0:128].rearrange("p (k t) -> p k t", k=8), AF.Copy, ["ps4"], ["yst"])

        vmemset(Sst, 0.0, ["Sst"])
        for b in range(8):
            c0 = b * 128
            rmsnorm_block(c0, 128, lambda kt: pv[:, PV_GMIX + kt:PV_GMIX + kt + 1],
                          lambda kt: hblk[:, kt, :], ["hblk"], ["xT", f"x{b}"])
            for i in range(8):
                pc = proj(16 + i)
                actf(cuT[:, i, :], ps[pc][:, :128], AF.Copy, [f"ps{pc}"], ["cuT"])
            ssm_block(False, True)
            if b == 7:
                for i in range(4):
                    pg = proj(4 + i)
                    actf(tA[:, 0:128], ps[pg][:, :128], AF.Sigmoid, [f"ps{pg}"], ["tA"])
                    pvv = proj(i)
                    tt(zbuf[:, i, 30:158], ps[pvv][:, :128], tA[:, 0:128], ALU.mult, [f"ps{pvv}", "tA"], ["zbuf"])
        ts(sendb[:, 0:64], Sst, cmask[:, 0:1], None, ALU.mult, None, ["Sst", "cmask"], ["sendb"])
        ts(sendb[:, 64:184].rearrange("p (k t) -> p k t", k=4), zbuf[:, :, 128:158], cmask[:, 0:1], None, ALU.mult, None,
           ["zbuf", "cmask"], ["sendb"])
        dma(G_, bin_d[l], sendb[:], ["sendb"], [f"bin{l}"])
        if not USE_CC:
            dma(G_, bout_d[l], bin_d[l], [f"bin{l}"], [f"bout{l}"])
        else:
          S.add(G_, lambda e, l=l: e.collective_compute("AllReduce", ALU.add, replica_groups=[[0, 1], [2, 3], [4, 5], [6, 7]],
                                                      ins=[bin_d[l].opt()], outs=[bout_d[l].opt()]),
                reads=[f"bin{l}"], writes=[f"bout{l}"], cc=True)
        dma(SY, recvb[:], bout_d[l], [f"bout{l}"], ["recvb"])
        ts(Sst, recvb[:, 0:64], cmask[:, 1:2], None, ALU.mult, None, ["recvb", "cmask"], ["Sst"])
        ts(zbuf[:, :, 0:30], recvb[:, 64:184].rearrange("p (k t) -> p k t", k=4), cmask[:, 1:2], None, ALU.mult, None,
           ["recvb", "cmask"], ["zbuf"])

        for b in range(NB):
            smp = (b == 8)
            c0 = b * 128
            if smp:
                dma(SY, bsb, bs_d[l, :, 512:1024], (), ["bsb"])
            xk = [f"x{b}"] if l > 0 or True else []
            rmsnorm_block(c0, 128, lambda kt: pv[:, PV_GMIX + kt:PV_GMIX + kt + 1],
                          lambda kt: hblk[:, kt, :], ["hblk"], ["xT", f"x{b}"])

            for i in range(4):
                pg = proj(4 + i)
                actf(tA[:, 0:128], ps[pg][:, :128], AF.Sigmoid, [f"ps{pg}"], ["tA"])
                pvv = proj(i)
                if smp:
                    tt(zs[:, i, :, 30:38], ps[pvv][:, :128].rearrange("p (s t) -> p s t", s=16),
                       tA[:, 0:128].rearrange("p (s t) -> p s t", s=16), ALU.mult, [f"ps{pvv}", "tA"], ["zs"])
                else:
                    tt(zbuf[:, i, 30:158], ps[pvv][:, :128], tA[:, 0:128], ALU.mult, [f"ps{pvv}", "tA"], ["zbuf"])
            for i in range(4):
                pu = proj(8 + i)
                actf(ub[:, i, :], ps[pu][:, :128], AF.Copy, [f"ps{pu}"], ["ub"])
            for i in range(4):
                proj(12 + i, swap=True, out_ap=ps[6][:, i * 128:(i + 1) * 128])
            for i in range(8):
                pc = proj(16 + i)
                actf(cuT[:, i, :], ps[pc][:, :128], AF.Copy, [f"ps{pc}"], ["cuT"])

            if l == 0 and b in (0, 8):
                tap(f"hblk{b}", hblk[:], ["hblk"])
                tap(f"cuT{b}", cuT[:], ["cuT"])
                tap(f"ub{b}", ub[:], ["ub"])
                tap(f"z{b}", zs[:] if smp else zbuf[:], ["zs" if smp else "zbuf"])
                if b == 0:
                    tap("bbp", bbp[:], ["bbp"]); tap("AR", AR, ["AR"]); tap("AI2", AI2, ["AI2"]); tap("wsb", wsb, ["wsb"])
            ssm_block(smp, False)
            if smp:
                dma(SY, ssms_o[l], Sso[:].rearrange("p k s -> p (k s)"), ["Sso"], [])
            if b == 7:
                dma(SY, ssmp_o[l], Sst, ["Sst"], [])
            for Tt in range(8):
                stt(yst[:, Tt, :], cuT[:, Tt, :], pv[:, PV_DSK + Tt:PV_DSK + Tt + 1], yst[:, Tt, :], ALU.mult, ALU.add,
                    ["cuT", "pv", "yst"], ["yst"])
            yf = yst[:].rearrange("p k t -> p (k t)")
            tt(tB, yf, yf, ALU.mult, ["yst"], ["tB"])
            ts(tB, tB, 0.044715 * 1.5957691216, 1.5957691216, ALU.mult, ALU.add, ["tB"], ["tB"])
            tt(tB, tB, yf, ALU.mult, ["tB", "yst"], ["tB"])
            actf(tB, tB, AF.Sigmoid, ["tB"], ["tB"])
            tt(zg[:].rearrange("p k t -> p (k t)"), yf, tB, ALU.mult, ["yst", "tB"], ["zg"])
            for Tt in range(8):
                mm(ps[5][:, 0:128], wgb[:, Tt, :], zg[:, Tt, :], True, True, ["wgb", "zg"], ["ps5"])
                mm(ps[5][:, 128:256], wgb[:, 8 + Tt, :], zg[:, Tt, :], True, True, ["wgb", "zg"], ["ps5"])
                actf(tC[:, 0:128], ps[5][:, 128:256], AF.Sigmoid, ["ps5", "pv"], ["tC"],
                     bias=pv[:, PV_BG + Tt:PV_BG + Tt + 1])
                stt(mix[:, 8 + Tt, :], ps[5][:, 0:128], pv[:, PV_BV + Tt:PV_BV + Tt + 1], tC[:, 0:128], ALU.add, ALU.mult,
                    ["ps5", "pv", "tC"], ["mix"])

            for i in range(4):
                acc = tA[:, 0:128]
                acc_v = acc.rearrange("p (s t) -> p s t", s=16) if smp else acc
                for k in range(31):
                    src = zs[:, i, :, k:k + 8] if smp else zbuf[:, i, k:k + 128]
                    wk = pv[:, PV_CW + i * 31 + k:PV_CW + i * 31 + k + 1]
                    if k == 0:
                        ts(acc_v, src, wk, None, ALU.mult, None, ["zs" if smp else "zbuf", "pv"], ["tA"])
                    else:
                        stt(acc_v, src, wk, acc_v, ALU.mult, ALU.add, ["zs" if smp else "zbuf", "pv", "tA"], ["tA"], ses=False)
                ts(acc, acc, pv[:, PV_CB + i:PV_CB + i + 1], None, ALU.add, None, ["tA", "pv"], ["tA"])
                sq = tA[:, 128:256]
                tt(sq, acc, acc, ALU.mult, ["tA"], ["tA2"])
                mm(ps[5][:, 0:128], ones_f[:], acc, True, True, ["ones_f", "tA"], ["ps5"])
                mm(ps[5][:, 128:256], ones_f[:], sq, True, True, ["ones_f", "tA2"], ["ps5"])
                m2 = tA[:, 256:384]; var = tA[:, 384:512]
                msb = tC[:, 128:256]
                actf(msb, ps[5][:, 0:128], AF.Copy, ["ps5"], ["tCm"])
                tt(m2, msb, msb, ALU.mult, ["tCm"], ["tA3"])
                tt(var, ps[5][:, 128:256], m2, ALU.subtract, ["ps5", "tA3"], ["tA4"])
                rsqrt(var, var, 1.0, ["tA4"], ["tA4"])
                tt(m2, acc, msb, ALU.subtract, ["tA", "tCm"], ["tA3"])
                tt(m2, m2, var, ALU.mult, ["tA3", "tA4"], ["tA3"])
                actf(mix[:, i, :], m2, AF.Silu, ["tA3", "pv"], ["mix"],
                     scale=pv[:, PV_LAG + i:PV_LAG + i + 1], bias=pv[:, PV_LAB + i:PV_LAB + i + 1])
            if smp:
                dma(SY, convs_o[l].rearrange("p (k s t) -> p k s t", k=4, s=16), zs[:, :, :, 8:38], ["zs"], [])
            else:
                if b == 7:
                    dma(SY, convp_o[l].rearrange("p (k t) -> p k t", k=4), zbuf[:, :, 128:158], ["zbuf"], [])
                vcopy(tC[:, 0:120].rearrange("p (k t) -> p k t", k=4), zbuf[:, :, 128:158], ["zbuf"], ["tC"])
                vcopy(zbuf[:, :, 0:30], tC[:, 0:120].rearrange("p (k t) -> p k t", k=4), ["tC"], ["zbuf"])

            s1, s2 = small[:, 0:1], small[:, 1:2]
            actf(junk, ps[6][:], AF.Copy, ["ps6"], ["junk", "small"], accum=s1)
            actf(junk, ps[6][:], AF.Square, ["ps6"], ["junk", "small"], accum=s2)
            mean, msq, varv = small[:, 2:3], small[:, 3:4], small[:, 4:5]
            ts(mean, s1, 1.0 / 512, None, ALU.mult, None, ["small"], ["small"])
            tt(msq, mean, mean, ALU.mult, ["small"], ["small"])
            stt(varv, s2, 1.0 / 512, msq, ALU.mult, ALU.subtract, ["small"], ["small"])
            rsqrt(varv, varv, 1.0, ["small"], ["small"])
            ts(vnf, ps[6][:], mean, None, ALU.subtract, None, ["ps6", "small"], ["vnf"])
            ts(vnf, vnf, varv, None, ALU.mult, None, ["vnf", "small"], ["vnf"])
            tt(vnf, vnf, lnv[:, 0:512], ALU.mult, ["vnf", "lnv"], ["vnf"])
            tt(vnf, vnf, lnv[:, 512:1024], ALU.add, ["vnf", "lnv"], ["vnf"])
            actf(vnb, vnf, AF.Copy, ["vnf"], ["vnb"])
            if smp:
                dma(SY, vs_o[l], vnf, ["vnf"], [])
            wo = 512 if smp else 0
            for h in range(4):
                mm(ps[5][:, h * 128:(h + 1) * 128], vnb[:, h * 128:(h + 1) * 128], wsb[:, wo + h * 128:wo + (h + 1) * 128],
                   True, True, ["vnb", "wsb"], ["ps5"])
            for h in range(4):
                tt(tC[:, 0:128], ps[5][:, h * 128:(h + 1) * 128], bsb[:, h * 128:(h + 1) * 128], ALU.add,
                   ["ps5", "bsb"], ["tC"])
                tt(mix[:, 4 + h, :], tC[:, 0:128], ub[:, h, :], ALU.mult, ["tC", "ub"], ["mix"])

            if l == 0 and b in (0, 8):
                tap(f"yst{b}", yst[:], ["yst"])
                tap(f"vnf{b}", vnf, ["vnf"])
                tap(f"mix{b}", mix[:], ["mix"])
            for ft in range(16):
                wi = load_w(w_out_d[l, ft])
                p = next_ps()
                for kt in range(16):
                    mm(ps[p][:, :128], win[wi][:, kt, :], mix[:, kt, :], kt == 0, kt == 15, ["mix", f"win{wi}"], [f"ps{p}"])
                tt(xT[:, ft, c0:c0 + 128], xT[:, ft, c0:c0 + 128], ps[p][:, :128], ALU.add, [f"ps{p}", f"x{b}", "xT"], [f"x{b}"])

        if l == 0:
            tap("xmid", xT[:, :, 0:128], ["xT", "x0"])
            tap("xmid8", xT[:, :, 1024:1152], ["xT", "x8"])
        allk = ["hblk", "mix", "xsq", "cuT", "vnb", "hbf", "zg", "wsb", "wgb", "bbp", "cpd", "zbuf", "zs", "ub", "vnf", "junk",
                "bu", "yst", "tA", "tA2", "tA3", "tA4", "tB", "tBa", "tBb", "tC", "tCa", "tCb", "tCm", "lnv", "wsf", "bsb", "Sst", "Ssm",
                "Sso", "AR", "AI2", "AR4", "AI4", "rt1", "rt2", "sS", "g0", "g1", "g2", "g3", "g4", "gin0", "gin1", "gin2"]
        ffk = ["hT", "act", "wgu0", "wgu1", "wdn0", "wdn1", "sgt"]
        for e in ENGS:
            S.add(e, lambda eng: eng.nop(), reads=(), writes=allk + ffk)
        for b in range(NB):
            c0 = b * 128
            rmsnorm_block(c0, 128, lambda kt: pv[:, PV_GFFN + kt:PV_GFFN + kt + 1],
                          lambda kt, c0=c0: hT_all[:, kt, c0:c0 + 128], ["hT"], ["xT", f"x{b}"])
        TBS = [(0, 512), (512, 512), (1024, 128)]
        gi = 0; di = 0
        for fb in range(4):
            for j in range(11):
                ff = fb * 11 + j
                gs = gi % 2; gi += 1
                dma(G_, wgu[gs][:].rearrange("p k c -> p (k c)"), w_gu_d[l, ff], (), [f"wgu{gs}"])
                for (t0, n) in TBS:
                    pa = next_ps(); pb = next_ps()
                    for kt in range(16):
                        mm(ps[pa][:, :n], wgu[gs][:, kt, 0:128], hT_all[:, kt, t0:t0 + n], kt == 0, kt == 15,
                           ["hT", f"wgu{gs}"], [f"ps{pa}"])
                    for kt in range(16):
                        mm(ps[pb][:, :n], wgu[gs][:, kt, 128:256], hT_all[:, kt, t0:t0 + n], kt == 0, kt == 15,
                           ["hT", f"wgu{gs}"], [f"ps{pb}"])
                    actf(sgt[:, :n], ps[pa][:, :n], AF.Silu, [f"ps{pa}"], ["sgt"])
                    tt(act[:, j, t0:t0 + n], sgt[:, :n], ps[pb][:, :n], ALU.mult, ["sgt", f"ps{pb}"], ["act"])
            for ft in range(16):
                ds_ = di % 2; di += 1
                dma(G_, wdn[ds_][:].rearrange("p k c -> p (k c)"), w_dn_d[l, fb * 16 + ft], (), [f"wdn{ds_}"])
                for ti, (t0, n) in enumerate(TBS):
                    p = next_ps()
                    for j in range(11):
                        mm(ps[p][:, :n], wdn[ds_][:, j, :], act[:, j, t0:t0 + n], j == 0, j == 10, ["act", f"wdn{ds_}"], [f"ps{p}"])
                    xks = [f"x{bb}" for bb in range(t0 // 128, (t0 + n) // 128)]
                    tt(xT[:, ft, t0:t0 + n], xT[:, ft, t0:t0 + n], ps[p][:, :n], ALU.add, [f"ps{p}", "xT"] + xks, xks)
        for e in ENGS:
            S.add(e, lambda eng: eng.nop(), reads=(), writes=allk + ffk)

    tap("xend", xT[:, :, 0:128], ["xT", "x0"])
    yv = yT_o.rearrange("p (k t) -> p k t", k=16)
    for b in range(NB):
        c0 = b * 128
        ob = R[:, 8192 + (b % 2) * 4096:8192 + (b % 2) * 4096 + 4096].bitcast(F32).rearrange("p (k t) -> p k t", k=16)
        rmsnorm_block(c0, 128, lambda kt: gfin[:, kt:kt + 1], lambda kt, ob=ob: ob[:, kt, :], [f"ob{b % 2}"], ["xT", f"x{b}"])
        dma(SY, yv[:, :, c0:c0 + 128], ob, [f"ob{b % 2}"], [])

    with nc.Block() as block:
        S.emit(nc, block, st)
    st.close()
    return nc


_NC = None


def _prep(inputs):
    f = lambda a: np.ascontiguousarray(np.asarray(a, dtype=np.float32))
    x_prompt = f(inputs["x_prompt"]); x_sample = f(inputs["x_sample"])
    sc = f(inputs["state_conv"]); sre = f(inputs["state_ssm_re"]); sim = f(inputs["state_ssm_im"])
    L = DEPTH
    w_in = f(inputs["w_in"]).reshape(L, 16, 128, 24, 128).transpose(0, 3, 2, 1, 4).reshape(L, 24, 128, 2048)
    w_out = f(inputs["w_out"]).reshape(L, 16, 128, 16, 128).transpose(0, 3, 2, 1, 4).reshape(L, 16, 128, 2048)
    wgu = f(inputs["w_gu"]).reshape(L, 16, 128, 2, 44, 128).transpose(0, 4, 2, 1, 3, 5).reshape(L, 44, 128, 4096)
    wdn = f(inputs["w_down"]).reshape(L, 4, 11, 128, 16, 128).transpose(0, 1, 4, 3, 2, 5).reshape(L, 64, 128, 1408)
    w_in = np.ascontiguousarray(w_in); w_out = np.ascontiguousarray(w_out)
    wgu = np.ascontiguousarray(wgu); wdn = np.ascontiguousarray(wdn)
    pvec = np.zeros((L, 128, 256), np.float32)
    col = lambda v, k: v.reshape(k, 128).T
    lnv = np.zeros((L, 128, 1024), np.float32); ws = np.zeros((L, 128, 1024), np.float32)
    bs = np.zeros((L, 128, 1024), np.float32)
    ssmR = np.zeros((L, 3, 128, 4096), np.float32); bpad = np.zeros((L, 2, 128, 4096), np.float32)
    cpad = np.zeros((L, 128, 64, 128), np.float32); ssmS = np.zeros((L, 128, 768), np.float32)
    wg = np.zeros((L, 128, 16, 128), np.float32)
    for l in range(L):
        pvec[l, :, 0:16] = col(f(inputs["g_mix"])[l], 16)
        pvec[l, :, 16:32] = col(f(inputs["g_ffn"])[l], 16)
        pvec[l, :, 32:156] = f(inputs["conv_w"])[l].reshape(31, 4, 128).transpose(2, 1, 0).reshape(128, 124)
        pvec[l, :, 156:160] = col(f(inputs["conv_b"])[l], 4)
        pvec[l, :, 160:164] = col(f(inputs["ln_a_g"])[l], 4)
        pvec[l, :, 164:168] = col(f(inputs["ln_a_b"])[l], 4)
        pvec[l, :, 168:176] = f(inputs["d_skip"])[l].reshape(8, 128).T
        bgl = f(inputs["b_glu"])[l]
        pvec[l, :, 176:184] = bgl[:, :16].reshape(8, 128).T
        pvec[l, :, 184:192] = bgl[:, 16:].reshape(8, 128).T
        lnv[l, :, 0:512] = f(inputs["ln_v_g"])[l][None, :]
        lnv[l, :, 512:1024] = f(inputs["ln_v_b"])[l][None, :]
        wsl = f(inputs["w_s"])[l]
        ws[l, :, 0:512] = wsl.transpose(2, 0, 1).reshape(128, 512)
        blk = np.zeros((128, 4, 128), np.float32)
        for qd in range(16):
            blk[8 * qd:8 * qd + 8, :, 8 * qd:8 * qd + 8] = wsl[:, 0:8, 0:8].transpose(2, 0, 1)
        ws[l, :, 512:1024] = blk.reshape(128, 512)
        bsl = f(inputs["b_s"])[l]
        bs[l, :, 0:512] = bsl.reshape(1, 512)
        bs[l, :, 512:1024] = np.tile(bsl[:, 0:8], (1, 16)).reshape(1, 512)
        are = f(inputs["a_re"])[l]; aim = f(inputs["a_im"])[l]; ldt = f(inputs["log_dt"])[l]
        ssmR[l, 0] = are.reshape(1, 4096); ssmR[l, 1] = aim.reshape(1, 4096)
        ssmR[l, 2] = np.repeat(ldt, 64).reshape(1, 4096)
        ssmS[l, :, 0:32] = are.reshape(32, 2, 64).transpose(1, 2, 0).reshape(128, 32)
        ssmS[l, :, 256:288] = aim.reshape(32, 2, 64).transpose(1, 2, 0).reshape(128, 32)
        ssmS[l, :, 512:544] = np.broadcast_to(ldt.reshape(32, 2).T[:, None, :], (2, 64, 32)).reshape(128, 32)
        bre = f(inputs["b_re"])[l]; bim = f(inputs["b_im"])[l]
        cre = f(inputs["c_re"])[l]; cim = f(inputs["c_im"])[l]
        wgl = f(inputs["w_glu"])[l]
        bp = np.zeros((2, 128, 32, 2, 64), np.float32)
        cp = np.zeros((2, 64, 64, 128), np.float32)
        for g in range(64):
            q, g2 = divmod(g, 2); g8 = g % 8; T = g // 8
            bp[0, 16 * g8:16 * g8 + 16, q, g2, :] = bre[g].T
            bp[1, 16 * g8:16 * g8 + 16, q, g2, :] = bim[g].T
            cp[g2, :, q, 16 * g8:16 * g8 + 16] = cre[g].T
            cp[g2, :, 32 + q, 16 * g8:16 * g8 + 16] = cim[g].T
            wg[l, 16 * g8:16 * g8 + 16, T, 16 * g8:16 * g8 + 16] = wgl[g][:, :16]
            wg[l, 16 * g8:16 * g8 + 16, 8 + T, 16 * g8:16 * g8 + 16] = wgl[g][:, 16:]
        bpad[l] = bp.reshape(2, 128, 4096)
        cpad[l] = cp.reshape(128, 64, 128)
    mask = np.tile(np.triu(np.ones((128, 128), np.float32))[:, None, :], (1, 4, 1)).reshape(128, 512)
    gfin = col(f(inputs["g_final"]), 16)
    shared = dict(w_in=w_in, w_out=w_out, w_gu=wgu, w_dn=wdn, pvec=pvec, gfin=np.ascontiguousarray(gfin),
                  lnv=lnv, ws=ws, bs=bs, mask=mask, ssmR=ssmR, bpad=bpad, cpad=cpad.reshape(L, 128, 8192), ssmS=ssmS,
                  wg=wg.reshape(L, 128, 2048))
    in_maps = []
    for c in range(NCORES):
        pr, half = divmod(c, 2)
        xin = np.concatenate([x_prompt[pr, half * LT:(half + 1) * LT], x_sample[16 * c:16 * c + 16].reshape(128, D)], axis=0)
        xT = np.ascontiguousarray(xin.T.reshape(16, 128, NT).transpose(1, 0, 2)).reshape(128, 16 * NT)
        zs0 = np.ascontiguousarray(sc[:, 16 * c:16 * c + 16].reshape(L, 16, 30, 4, 128).transpose(0, 4, 3, 1, 2)).reshape(L, 128, 1920)
        def sl(a):
            return a[:, 16 * c:16 * c + 16].reshape(L, 16, 32, 2, 64).transpose(0, 3, 4, 2, 1).reshape(L, 128, 32, 16)
        ss0 = np.ascontiguousarray(np.stack([sl(sre), sl(sim)], axis=2)).reshape(L, 128, 1024)
        cm = np.zeros((128, 2), np.float32); cm[:, 0] = 1.0 - half; cm[:, 1] = float(half)
        m = dict(shared); m.update(xT=xT, zs0=zs0, ss0=ss0, cmask=cm)
        in_maps.append(m)
    return in_maps


def kernel(**inputs):
    global _NC
    if _NC is None:
        _NC = build_program()
    in_maps = _prep(inputs)
    res = run_bass_kernel_spmd(_NC, in_maps, core_ids=list(range(NCORES)))
    R = res.results
    L = DEPTH
    y_prompt = np.zeros((4, 2048, D), np.float32); y_sample = np.zeros((128, 8, D), np.float32)
    conv_p = np.zeros((L, 4, 30, 512), np.float32)
    ssr_p = np.zeros((L, 4, 64, 64), np.float32); ssi_p = np.zeros((L, 4, 64, 64), np.float32)
    conv_s = np.zeros((L, 128, 30, 512), np.float32)
    ssr_s = np.zeros((L, 128, 64, 64), np.float32); ssi_s = np.zeros((L, 128, 64, 64), np.float32)
    v_s = np.zeros((L, 128, 8, 512), np.float32)
    for c in range(NCORES):
        pr, half = divmod(c, 2)
        y = np.asarray(R[c]["yT"]).reshape(128, 16, NT).transpose(2, 1, 0).reshape(NT, D)
        y_prompt[pr, half * LT:(half + 1) * LT] = y[:LT]
        y_sample[16 * c:16 * c + 16] = y[LT:].reshape(16, 8, D)
        cs = np.asarray(R[c]["conv_s"]).reshape(L, 128, 4, 16, 30).transpose(0, 3, 4, 2, 1).reshape(L, 16, 30, 512)
        conv_s[:, 16 * c:16 * c + 16] = cs
        ss = np.asarray(R[c]["ssm_s"]).reshape(L, 2, 64, 2, 32, 16)
        ss = ss.transpose(0, 3, 5, 4, 1, 2).reshape(L, 2, 16, 64, 64)
        ssr_s[:, 16 * c:16 * c + 16] = ss[:, 0]; ssi_s[:, 16 * c:16 * c + 16] = ss[:, 1]
        v_s[:, 16 * c:16 * c + 16] = np.asarray(R[c]["v_s"]).reshape(L, 16, 8, 512)
        if half == 1:
            conv_p[:, pr] = np.asarray(R[c]["conv_p"]).reshape(L, 128, 4, 30).transpose(0, 3, 2, 1).reshape(L, 30, 512)
            sp = np.asarray(R[c]["ssm_p"]).reshape(L, 2, 64, 2, 32).transpose(0, 3, 4, 1, 2).reshape(L, 2, 64, 64)
            ssr_p[:, pr] = sp[:, 0]; ssi_p[:, pr] = sp[:, 1]
    return (y_prompt, y_sample, conv_p, ssr_p, ssi_p, conv_s, ssr_s, ssi_s, v_s)
```

```python
import contextlib
import numpy as np
import concourse.bass as bass
import concourse.mybir as mybir
from concourse.bass_utils import run_bass_kernel_spmd

F32 = mybir.dt.float32
BF16 = mybir.dt.bfloat16
ALU = mybir.AluOpType
AF = mybir.ActivationFunctionType

D = 2048; NT = 1152; LT = 1024; NB = 9; DEPTH = 2; DFF = 5632
EPS = 1e-6
ENGS = ["tensor", "vector", "scalar", "gpsimd", "sync"]
NCORES = 8


class Ins:
    __slots__ = ("eng", "fn", "dma", "deps", "needed", "sig", "idx", "cc", "ses")

    def __init__(self, eng, fn, dma):
        self.eng = eng; self.fn = fn; self.dma = dma; self.deps = []; self.needed = False; self.sig = None; self.cc = False; self.ses = True


class Sched:
    def __init__(self):
        self.instrs = {e: [] for e in ENGS}
        self.last_w = {}
        self.readers = {}
        self.n = 0

    def add(self, eng, fn, reads=(), writes=(), dma=False, cc=False, ses=True):
        ins = Ins(eng, fn, dma or cc)
        ins.cc = cc; ins.ses = ses
        ins.idx = self.n; self.n += 1
        deps = {}
        for k in reads:
            w = self.last_w.get(k)
            if w is not None: deps[w.idx] = w
        for k in writes:
            w = self.last_w.get(k)
            if w is not None: deps[w.idx] = w
            for r in self.readers.get(k, ()): deps[r.idx] = r
        ins.deps = list(deps.values())
        for d in ins.deps: d.needed = True
        for k in reads: self.readers.setdefault(k, []).append(ins)
        for k in writes:
            self.last_w[k] = ins; self.readers[k] = []
        self.instrs[eng].append(ins)
        return ins

    def emit(self, nc, block, stack):
        SEG = 3000
        NDS = 6
        esems = {}
        for e in ENGS:
            ncomp = sum(1 for i in self.instrs[e] if (not i.dma) and i.needed)
            esems[e] = [stack.enter_context(nc.semaphore(f"e_{e}_{k}")) for k in range(ncomp // SEG + 1)]
            cnt = 0
            for i in self.instrs[e]:
                if (not i.dma) and i.needed:
                    i.sig = (esems[e][cnt // SEG], cnt % SEG + 1); cnt += 1
        dsems = {e: [stack.enter_context(nc.semaphore(f"d_{e}_{k}")) for k in range(NDS)] for e in ENGS
                 if any(i.dma for i in self.instrs[e])}
        for e in ENGS:
            if e not in dsems: continue
            k = 0; tot = [0] * NDS
            ccsem = None; cccnt = 0
            for i in self.instrs[e]:
                if i.cc:
                    if ccsem is None: ccsem = stack.enter_context(nc.semaphore(f"cc_{e}"))
                    cccnt += 1
                    i.sig = (ccsem, cccnt, -1)
                elif i.dma:
                    s = k % NDS; k += 1; tot[s] += 16
                    i.sig = (dsems[e][s], tot[s], s)

        def stream(e):
            def body(eng):
                waited = {}
                prev_on = {}

                def wait(sem, val):
                    key = id(sem)
                    if waited.get(key, 0) >= val: return
                    eng.wait_ge(sem, val); waited[key] = val
                for ins in self.instrs[e]:
                    for d in ins.deps:
                        if d.dma:
                            wait(d.sig[0], d.sig[1])
                        elif d.eng != e or (e in SAME_ENG_SYNC and ins.ses):
                            wait(d.sig[0], d.sig[1])
                    if ins.cc:
                        ins.fn(eng).then_inc(ins.sig[0], 1)
                    elif ins.dma:
                        s = ins.sig[2]
                        if s in prev_on: wait(ins.sig[0], prev_on[s])
                        prev_on[s] = ins.sig[1]
                        ins.fn(eng).then_inc(ins.sig[0], 16)
                    else:
                        r = ins.fn(eng)
                        if ins.needed: r.then_inc(ins.sig[0], 1)
                for s, v in prev_on.items():
                    wait(dsems[e][s], v)
            return body
        for e in ENGS:
            if self.instrs[e]:
                getattr(block, e)(stream(e))


DEBUG_TAPS = False
USE_CC = True
SAME_ENG_SYNC = ("vector", "scalar", "gpsimd", "sync")
TAP_NAMES = []


def build_program():
    nc = bass.Bass("TRN2", target_bir_lowering=False)
    S = Sched()
    st = contextlib.ExitStack()

    def din(name, shape, dt=F32):
        return nc.dram_tensor(name, list(shape), dt, kind="ExternalInput").ap()

    def dout(name, shape):
        return nc.dram_tensor(name, list(shape), F32, kind="ExternalOutput").ap()

    def sb(name, shape, dt=F32):
        return st.enter_context(nc.sbuf_tensor(name, list(shape), dt))

    xT_d = din("xT", [128, 16 * NT])
    zs0_d = din("zs0", [DEPTH, 128, 4 * 16 * 30])
    ss0_d = din("ss0", [DEPTH, 128, 64 * 16])
    w_in_d = din("w_in", [DEPTH, 24, 128, 2048])
    w_out_d = din("w_out", [DEPTH, 16, 128, 2048])
    w_gu_d = din("w_gu", [DEPTH, 44, 128, 4096])
    w_dn_d = din("w_dn", [DEPTH, 64, 128, 1408])
    pv_d = din("pvec", [DEPTH, 128, 256])
    gfin_d = din("gfin", [128, 16])
    lnv_d = din("lnv", [DEPTH, 128, 1024])
    ws_d = din("ws", [DEPTH, 128, 1024])
    bs_d = din("bs", [DEPTH, 128, 1024])
    mask_d = din("mask", [128, 512])
    ssmR_d = din("ssmR", [DEPTH, 3, 128, 4096])
    bpad_d = din("bpad", [DEPTH, 2, 128, 4096])
    cpad_d = din("cpad", [DEPTH, 128, 64 * 128])
    ssmS_d = din("ssmS", [DEPTH, 128, 768])
    wg_d = din("wg", [DEPTH, 128, 2048])

    cmask_d = din("cmask", [128, 2])
    bin_d = [nc.dram_tensor(f"xch_in{l}", [128, 184], F32).ap() for l in range(DEPTH)]
    bout_d = [nc.dram_tensor(f"xch_out{l}", [128, 184], F32).ap() for l in range(DEPTH)]
    yT_o = dout("yT", [128, 16 * NT])
    convp_o = dout("conv_p", [DEPTH, 128, 120])
    ssmp_o = dout("ssm_p", [DEPTH, 128, 64])
    convs_o = dout("conv_s", [DEPTH, 128, 4 * 16 * 30])
    ssms_o = dout("ssm_s", [DEPTH, 128, 64 * 16])
    vs_o = dout("v_s", [DEPTH, 128, 512])

    xT = sb("xT_sb", [128, 16, NT])
    RSZ = 57600
    R = sb("R", [128, RSZ], BF16)
    pv = sb("pv", [128, 256])
    gfin = sb("gfin_sb", [128, 16])
    ones_b = sb("ones_b", [128, 128], BF16)
    ones_f = sb("ones_f", [128, 128])
    mask = sb("mask_sb", [128, 512], BF16)
    small = sb("small", [128, 64])
    epsc = sb("epsc", [128, 1])
    cmask = sb("cmask_sb", [128, 2])
    sendb = sb("sendb", [128, 184])
    recvb = sb("recvb", [128, 184])
    nint = sb("nint", [128, 256], mybir.dt.int32)
    rstd = sb("rstd", [128, 128])
    win = [sb(f"win{i}", [128, 16, 128], BF16) for i in range(2)]

    off = [0]

    def carve(n_bf16, dt=BF16, shape=None):
        a = off[0]; off[0] += n_bf16
        v = R[:, a:a + n_bf16]
        if dt == F32:
            v = v.bitcast(F32)
        return v

    hT_all = carve(16 * NT).rearrange("p (k t) -> p k t", k=16)
    act = carve(11 * NT).rearrange("p (k t) -> p k t", k=11)
    wgu = [carve(4096).rearrange("p (k c) -> p k c", k=16) for _ in range(2)]
    wdn = [carve(1408).rearrange("p (k c) -> p k c", k=11) for _ in range(2)]
    sgt = carve(1024, F32)
    ffn_end = off[0]
    off[0] = 0
    hblk = carve(2048).rearrange("p (k t) -> p k t", k=16)
    mix = carve(2048).rearrange("p (k t) -> p k t", k=16)
    cuT = carve(1024).rearrange("p (k t) -> p k t", k=8)
    vnb = carve(512)
    wsb = carve(1024)
    wgb = carve(2048).rearrange("p (k c) -> p k c", k=16)
    bbp = carve(8192).rearrange("p (k c) -> p k c", k=64)
    cpd = carve(8192).rearrange("p (k c) -> p k c", k=64)
    g_off = off[0]
    bu = carve(2 * 1024, F32).rearrange("p (k t) -> p k t", k=64)
    hbf = carve(1024).rearrange("p (k t) -> p k t", k=64)
    zg = carve(1024).rearrange("p (k t) -> p k t", k=8)
    G = [R[:, g_off + 512 * i:g_off + 512 * (i + 1)].bitcast(F32) for i in range(8)]
    bu_b = carve(2 * 1024, F32).rearrange("p (k t) -> p k t", k=64)
    zbuf = carve(2 * 4 * 158, F32).rearrange("p (k t) -> p k t", k=4)
    zs = carve(2 * 4 * 16 * 38, F32).rearrange("p (k s t) -> p k s t", k=4, s=16)
    ub = carve(2 * 512, F32).rearrange("p (k t) -> p k t", k=4)
    vnf = carve(2 * 512, F32)
    junk = vnf
    tA = carve(2 * 512, F32)
    tB = carve(2 * 1024, F32)
    wsf = tB
    tC = carve(2 * 512, F32)
    lnv = carve(2 * 1024, F32)
    bsb = carve(2 * 512, F32)
    Sst = carve(2 * 64, F32)
    Ssm = carve(2 * 1024, F32).rearrange("p (k s) -> p k s", k=64)
    Sso = Ssm
    AR = carve(2 * 64, F32)
    AI2 = carve(2 * 64, F32)
    AR4 = carve(2 * 256, F32).rearrange("p (k s) -> p k s", k=64)
    AI4 = carve(2 * 256, F32).rearrange("p (k s) -> p k s", k=64)
    rt1 = carve(2 * 256, F32)
    rt2 = carve(2 * 256, F32)
    sS = carve(2 * 768, F32)
    PWr = carve(2 * 512, F32).rearrange("p (k t) -> p k t", k=32)
    PWi = carve(2 * 512, F32).rearrange("p (k t) -> p k t", k=32)
    A16 = carve(2 * 64, F32)
    A16i2 = carve(2 * 64, F32)
    Evec = carve(2 * 64, F32)
    if off[0] < ffn_end: off[0] = ffn_end
    yst_raw = carve(2 * 1024)
    yst = yst_raw.bitcast(F32).rearrange("p (k t) -> p k t", k=8)
    xsq = yst_raw.rearrange("p (k t) -> p k t", k=16)
    mixer_end = off[0]
    print("R usage (bf16 elems)", ffn_end, mixer_end)
    assert max(ffn_end, mixer_end) <= RSZ, (ffn_end, mixer_end)

    ps = [st.enter_context(nc.psum_tensor(f"ps{i}", [128, 512], F32)) for i in range(8)]

    PV_GMIX, PV_GFFN, PV_CW, PV_CB, PV_LAG, PV_LAB, PV_DSK, PV_BV, PV_BG = 0, 16, 32, 156, 160, 164, 168, 176, 184

    V, A, T_, G_, SY = "vector", "scalar", "tensor", "gpsimd", "sync"

    def dma(eng, out, in_, r, w):
        S.add(eng, lambda e: e.dma_start(out=out, in_=in_), reads=r, writes=w, dma=True)

    def tt(out, a, b, op, r, w, ses=True):
        S.add(V, lambda e: e.tensor_tensor(out=out, in0=a, in1=b, op=op), reads=r, writes=w, ses=ses)

    def ts(out, a, s1, s2, op0, op1, r, w):
        if op1 is None:
            S.add(V, lambda e: e.tensor_scalar(out=out, in0=a, scalar1=s1, scalar2=None, op0=op0), reads=r, writes=w)
        else:
            S.add(V, lambda e: e.tensor_scalar(out=out, in0=a, scalar1=s1, scalar2=s2, op0=op0, op1=op1), reads=r, writes=w)

    def stt(out, a, s, b, op0, op1, r, w, ses=True):
        S.add(V, lambda e: e.scalar_tensor_tensor(out=out, in0=a, scalar=s, in1=b, op0=op0, op1=op1), reads=r, writes=w, ses=ses)

    def actf(out, in_, func, r, w, bias=None, scale=None, accum=None):
        kw = {}
        if bias is not None: kw["bias"] = bias
        if scale is not None: kw["scale"] = scale
        if accum is not None: kw["accum_out"] = accum
        S.add(A, lambda e: e.activation(out=out, in_=in_, func=func, **kw), reads=r, writes=w)

    def mm(out, lhsT, rhs, start, stop, r, w):
        S.add(T_, lambda e: e.matmul(out, lhsT, rhs, start=start, stop=stop), reads=r, writes=w)

    def rsqrt(out, in_, scale, r, w):
        npart = out.shape[0]
        actf(out, in_, AF.Sqrt, list(r) + ["epsc"], w, bias=epsc[0:npart, :], scale=scale)
        S.add(V, lambda e: e.reciprocal(out=out, in_=out), reads=w, writes=w)

    def tap(name, ap, r):
        if not DEBUG_TAPS: return
        shp = list(ap.shape)
        d = nc.dram_tensor("tap_" + name, shp, ap.dtype, kind="ExternalOutput").ap()
        TAP_NAMES.append("tap_" + name)
        dma(SY, d, ap, r, [])

    def vcopy(out, in_, r, w):
        S.add(V, lambda e: e.tensor_copy(out=out, in_=in_), reads=r, writes=w)

    def vmemset(ap, val, w):
        S.add(V, lambda e: e.memset(ap, val), reads=(), writes=w)

    dma(SY, xT[:].rearrange("p k t -> p (k t)"), xT_d, (), ["xT"])
    dma(SY, gfin[:], gfin_d, (), ["gfin"])
    dma(SY, cmask[:], cmask_d, (), ["cmask"])
    dma(G_, mask[:], mask_d, (), ["mask"])
    vmemset(ones_b[:], 1.0, ["ones_b"])
    vmemset(epsc[:], EPS, ["epsc"])
    vmemset(ones_f[:], 1.0 / 128.0, ["ones_f"])

    psi = [0]
    cslot = [0]

    def next_ps():
        i = psi[0] % 6; psi[0] += 1
        return i

    wsl = [0]

    def load_w(src):
        i = wsl[0] % 2; wsl[0] += 1
        dma(G_, win[i][:].rearrange("p k c -> p (k c)"), src, (), [f"win{i}"])
        return i

    def rmsnorm_block(c0, n, gcol, out_fn, out_keys, xkeys):
        actf(xsq[:, :, :n], xT[:, :, c0:c0 + n], AF.Square, xkeys, ["yst"])
        for kt in range(16):
            mm(ps[7][:, :n], ones_b[:], xsq[:, kt, :n], kt == 0, kt == 15, ["ones_b", "yst"], ["ps7"])
        rsqrt(rstd[:, :n], ps[7][:, :n], 1.0 / D, ["ps7"], ["rstd"])
        for kt in range(16):
            stt(out_fn(kt), xT[:, kt, c0:c0 + n], gcol(kt), rstd[:, :n], ALU.mult, ALU.mult,
                xkeys + ["rstd", "pv", "gfin"], out_keys)

    for l in range(DEPTH):
        dma(SY, pv[:], pv_d[l], (), ["pv"])
        dma(SY, lnv, lnv_d[l], (), ["lnv"])
        dma(SY, wsf, ws_d[l], (), ["wsf"])
        dma(SY, bsb, bs_d[l, :, 0:512], (), ["bsb"])
        dma(SY, sS, ssmS_d[l], (), ["sS"])
        dma(SY, zs[:, :, :, 0:30], zs0_d[l].rearrange("p (k s t) -> p k s t", k=4, s=16), (), ["zs"])
        dma(SY, Ssm[:].rearrange("p k s -> p (k s)"), ss0_d[l], (), ["Ssm"])
        dma(G_, wgb[:].rearrange("p k c -> p (k c)"), wg_d[l], (), ["wgb"])
        dma(G_, cpd[:].rearrange("p k c -> p (k c)"), cpad_d[l], (), ["cpd"])
        tt(wsb[:, 0:512], wsf[:, 0:512], mask[:], ALU.mult, ["wsf", "mask"], ["wsb"])
        tt(wsb[:, 512:1024], wsf[:, 512:1024], mask[:], ALU.mult, ["wsf", "mask"], ["wsb"])

        def abar_gen(are, aim, ldt, n, tmp, out_r, out_i, key_in, key_out, want_k=None):
            dt_, mg, ang, sn, cs = tmp[0], tmp[1], tmp[2], tmp[3], tmp[4]
            actf(dt_, ldt, AF.Exp, key_in, ["g0"])
            tt(mg, dt_, are, ALU.mult, ["g0"] + key_in, ["g1"])
            actf(mg, mg, AF.Exp, ["g1"], ["g1"])
            tt(ang, dt_, aim, ALU.mult, ["g0"] + key_in, ["g2"])
            nf = tB[:, 0:n]; ni = nint[:, 0:n]; mm_ = tC[:, 0:n]
            for (dst, phase, key) in ((sn, 0.0, "g3"), (cs, 0.5 * np.pi, "g4")):
                ts(dst, ang, phase, None, ALU.add, None, ["g2"], [key])
                ts(nf, dst, 1.0 / (2 * np.pi), None, ALU.mult, None, [key], ["tBa"])
                vcopy(ni, nf, ["tBa"], ["tBb"])
                vcopy(nf, ni, ["tBb"], ["tBa"])
                stt(dst, nf, -2 * np.pi, dst, ALU.mult, ALU.add, ["tBa", key], [key])
                ts(mm_, dst, np.pi, -2 * np.pi, ALU.is_gt, ALU.mult, [key], ["tCa"])
                tt(dst, dst, mm_, ALU.add, [key, "tCa"], [key])
                ts(mm_, dst, -np.pi, 2 * np.pi, ALU.is_lt, ALU.mult, [key], ["tCa"])
                tt(dst, dst, mm_, ALU.add, [key, "tCa"], [key])
                actf(dst, dst, AF.Sin, [key], [key])
            tt(out_r, mg, cs, ALU.mult, ["g1", "g4"], key_out)
            tt(out_i, mg, sn, ALU.mult, ["g1", "g3"], key_out)

        gt = [g[:] for g in G]
        abar_gen(sS[:, 0:256], sS[:, 256:512], sS[:, 512:768], 256, gt, tA[:, 0:256], tA[:, 256:512], ["sS"], ["tA"])
        vcopy(AR[:, 0:32], tA[:, 0:32], ["tA"], ["AR"])
        vcopy(AR[:, 32:64], tA[:, 0:32], ["tA"], ["AR"])
        vcopy(AI2[:, 32:64], tA[:, 256:288], ["tA"], ["AI2"])
        ts(AI2[:, 0:32], tA[:, 256:288], -1.0, None, ALU.mult, None, ["tA"], ["AI2"])
        ar_, ai_ = AR[:, 0:32], AI2[:, 32:64]
        vmemset(PWr[:, :, 15], 1.0, ["PW"])
        vmemset(PWi[:, :, 15], 0.0, ["PW"])
        def cmul(o_r, o_i, x_r, x_i, keys_in, keys_out):
            t1_, t2_ = rt1[:, 0:32], rt2[:, 0:32]
            tt(t1_, x_r, ar_, ALU.mult, keys_in + ["AR"], ["rt1"])
            tt(t2_, x_i, ai_, ALU.mult, keys_in + ["AI2"], ["rt2"])
            tt(rt1[:, 32:64], x_r, ai_, ALU.mult, keys_in + ["AI2"], ["rt1b"])
            tt(rt2[:, 32:64], x_i, ar_, ALU.mult, keys_in + ["AR"], ["rt2b"])
            tt(o_r, t1_, t2_, ALU.subtract, ["rt1", "rt2"], keys_out)
            tt(o_i, rt1[:, 32:64], rt2[:, 32:64], ALU.add, ["rt1b", "rt2b"], keys_out)
        for tau in range(14, -1, -1):
            cmul(PWr[:, :, tau], PWi[:, :, tau], PWr[:, :, tau + 1], PWi[:, :, tau + 1], ["PW"], ["PW"])
        cmul(A16[:, 0:32], A16i2[:, 32:64], PWr[:, :, 0], PWi[:, :, 0], ["PW"], ["A16"])
        vcopy(A16[:, 32:64], A16[:, 0:32], ["A16"], ["A16"])
        ts(A16i2[:, 0:32], A16i2[:, 32:64], -1.0, None, ALU.mult, None, ["A16"], ["A16"])
        for s4 in range(4):
            vcopy(AR4[:, :, s4], AR[:], ["AR"], ["AR4"])
            vcopy(AI4[:, :, s4], AI2[:], ["AI2"], ["AI4"])

        for ch in range(16):
            c0 = ch * 256
            for j in range(3):
                dma(SY, G[5 + j][:], ssmR_d[l, j, :, c0:c0 + 256], (), [f"gin{j}"])
            are, aim, ldt = G[5][:], G[6][:], G[7][:]
            gt = [g[:] for g in G]
            abr, abi = tA[:, 0:256], tA[:, 256:512]
            abar_gen(are, aim, ldt, 256, gt, abr, abi, ["gin0", "gin1", "gin2"], ["tA"])
            den, kr, ki, t0 = tB[:, 0:256], tB[:, 256:512], tC[:, 0:256], tC[:, 256:512]
            tt(den, are, are, ALU.mult, ["gin0"], ["tBa"])
            tt(t0, aim, aim, ALU.mult, ["gin1"], ["tCb"])
            tt(den, den, t0, ALU.add, ["tBa", "tCb"], ["tBa"])
            S.add(V, lambda e, den=den: e.reciprocal(out=den, in_=den), reads=["tBa"], writes=["tBa"])
            ts(abr, abr, -1.0, None, ALU.add, None, ["tA"], ["tA"])
            tt(kr, abr, are, ALU.mult, ["tA", "gin0"], ["tBb"])
            tt(t0, abi, aim, ALU.mult, ["tA", "gin1"], ["tCb"])
            tt(kr, kr, t0, ALU.add, ["tBb", "tCb"], ["tBb"])
            tt(kr, kr, den, ALU.mult, ["tBb", "tBa"], ["tBb"])
            tt(ki, abi, are, ALU.mult, ["tA", "gin0"], ["tCa"])
            tt(t0, abr, aim, ALU.mult, ["tA", "gin1"], ["tCb"])
            tt(ki, ki, t0, ALU.subtract, ["tCa", "tCb"], ["tCa"])
            tt(ki, ki, den, ALU.mult, ["tCa", "tBa"], ["tCa"])
            dma(SY, G[0][:], bpad_d[l, 0, :, c0:c0 + 256], (), ["g0"])
            dma(SY, G[1][:], bpad_d[l, 1, :, c0:c0 + 256], (), ["g1"])
            bre, bim = G[0][:], G[1][:]
            o_re = bbp[:, ch * 2:(ch + 1) * 2, :].rearrange("p k c -> p (k c)")
            o_im = bbp[:, 32 + ch * 2:32 + (ch + 1) * 2, :].rearrange("p k c -> p (k c)")
            tt(G[2][:], kr, bre, ALU.mult, ["tBb", "g0"], ["g2"])
            tt(G[3][:], ki, bim, ALU.mult, ["tCa", "g1"], ["g3"])
            tt(o_re, G[2][:], G[3][:], ALU.subtract, ["g2", "g3"], ["bbp"])
            tt(G[2][:], kr, bim, ALU.mult, ["tBb", "g1"], ["g2"])
            tt(G[3][:], ki, bre, ALU.mult, ["tCa", "g0"], ["g3"])
            tt(o_im, G[2][:], G[3][:], ALU.add, ["g2", "g3"], ["bbp"])

        def col_slot():
            i = cslot[0] % 2; cslot[0] += 1
            bank = (7, 5)[i]
            return ps[bank][:, 0:128], f"ps{bank}"

        def proj(ct, swap=False, out_ap=None):
            wi = load_w(w_in_d[l, ct])
            if out_ap is None:
                o, ok = col_slot()
            else:
                o, ok = out_ap, "ps6"
            for kt in range(16):
                if swap:
                    mm(o, hblk[:, kt, :], win[wi][:, kt, :], kt == 0, kt == 15, ["hblk", f"win{wi}"], [ok])
                else:
                    mm(o, win[wi][:, kt, :], hblk[:, kt, :], kt == 0, kt == 15, ["hblk", f"win{wi}"], [ok])
            return o, ok


        def ssm_block(smp, pass1, thunks=()):
            BU = [bu, bu_b]
            thunks = list(thunks)

            def emit_bu(sub, buf):
                t0 = sub * 16
                for bank in range(2):
                    pb = 2 * buf + bank
                    for j in range(32):
                        idx = bank * 32 + j
                        q = idx % 32
                        mm(ps[pb][:, j * 16:(j + 1) * 16], bbp[:, idx, :], cuT[:, q // 4, t0:t0 + 16], True, True,
                           ["bbp", "cuT"], [f"ps{pb}"])
                    actf(BU[buf][:, bank * 32:(bank + 1) * 32, :], ps[pb][:].rearrange("p (k t) -> p k t", k=32), AF.Copy,
                         [f"ps{pb}"], [f"bu{buf}"])

            emit_bu(0, 0)
            for sub in range(8):
                t0 = sub * 16
                buf = sub % 2
                bt = BU[buf]; bk = f"bu{buf}"
                if sub + 1 < 8:
                    emit_bu(sub + 1, 1 - buf)
                nth = -(-len(thunks) // (8 - sub))
                for _ in range(nth):
                    thunks.pop(0)()
                if pass1:
                    br_, bi_ = bt[:, 0:32, :], bt[:, 32:64, :]
                    ta_ = tB[:, 0:512].rearrange("p (k t) -> p k t", k=32)
                    tb_ = tB[:, 512:1024].rearrange("p (k t) -> p k t", k=32)
                    X = mybir.AxisListType.X
                    tt(ta_, PWr, br_, ALU.mult, ["PW", bk], ["tBa"], ses=False)
                    tt(tb_, PWi, bi_, ALU.mult, ["PW", bk], ["tBb"], ses=False)
                    tt(ta_, ta_, tb_, ALU.subtract, ["tBa", "tBb"], ["tBa"], ses=False)
                    S.add(V, lambda e, ta_=ta_: e.tensor_reduce(out=Evec[:, 0:32], in_=ta_, axis=X, op=ALU.add),
                          reads=["tBa"], writes=["Evec"])
                    tt(ta_, PWr, bi_, ALU.mult, ["PW", bk], ["tBa"], ses=False)
                    tt(tb_, PWi, br_, ALU.mult, ["PW", bk], ["tBb"], ses=False)
                    tt(ta_, ta_, tb_, ALU.add, ["tBa", "tBb"], ["tBa"], ses=False)
                    S.add(V, lambda e, ta_=ta_: e.tensor_reduce(out=Evec[:, 32:64], in_=ta_, axis=X, op=ALU.add),
                          reads=["tBa"], writes=["Evec"])
                    r1, r2 = rt1[:, 0:64], rt2[:, 0:64]
                    tt(r1, A16, Sst, ALU.mult, ["A16", "Sst"], ["rt1"])
                    tt(r2[:, 0:32], A16i2[:, 0:32], Sst[:, 32:64], ALU.mult, ["A16", "Sst"], ["rt2"])
                    tt(r2[:, 32:64], A16i2[:, 32:64], Sst[:, 0:32], ALU.mult, ["A16", "Sst"], ["rt2"])
                    tt(r1, r1, r2, ALU.add, ["rt1", "rt2"], ["rt1"])
                    tt(Sst, r1, Evec, ALU.add, ["rt1", "Evec"], ["Sst"])
                    continue
                if not smp:
                    for t in range(16):
                        prev = Sst if t == 0 else bt[:, :, t - 1]
                        pk = ["Sst"] if t == 0 else [bk]
                        r1, r2 = rt1[:, 0:64], rt2[:, 0:64]
                        s0 = (t == 0)
                        tt(r1, AR, prev, ALU.mult, ["AR"] + pk, ["rt1"], ses=s0)
                        tt(r2[:, 0:32], AI2[:, 0:32], prev[:, 32:64], ALU.mult, ["AI2"] + pk, ["rt2"], ses=s0)
                        tt(r2[:, 32:64], AI2[:, 32:64], prev[:, 0:32], ALU.mult, ["AI2"] + pk, ["rt2"], ses=s0)
                        tt(r1, r1, r2, ALU.add, ["rt1", "rt2"], ["rt1"], ses=False)
                        tt(bt[:, :, t], bt[:, :, t], r1, ALU.add, [bk, "rt1"], [bk], ses=False)
                    vcopy(Sst, bt[:, :, 15], [bk], ["Sst"])
                else:
                    bu4 = bt[:].rearrange("p k (s t) -> p k s t", s=2)
                    for t in range(8):
                        prev = Ssm[:, :, sub * 2:(sub + 1) * 2] if t == 0 else bu4[:, :, :, t - 1]
                        pk = ["Ssm"] if t == 0 else [bk]
                        r1 = rt1[:, 0:128].rearrange("p (k s) -> p k s", k=64)
                        r2 = rt2[:, 0:128].rearrange("p (k s) -> p k s", k=64)
                        tt(r1, AR4[:, :, 0:2], prev, ALU.mult, ["AR4"] + pk, ["rt1"], ses=False)
                        tt(r2[:, 0:32, :], AI4[:, 0:32, 0:2], prev[:, 32:64, :], ALU.mult, ["AI4"] + pk, ["rt2"], ses=False)
                        tt(r2[:, 32:64, :], AI4[:, 32:64, 0:2], prev[:, 0:32, :], ALU.mult, ["AI4"] + pk, ["rt2"], ses=False)
                        tt(r1, r1, r2, ALU.add, ["rt1", "rt2"], ["rt1"], ses=False)
                        tt(bu4[:, :, :, t], bu4[:, :, :, t], r1, ALU.add, [bk, "rt1"], [bk], ses=False)
                    vcopy(Sso[:, :, sub * 2:(sub + 1) * 2], bu4[:, :, :, 7], [bk], ["Sso"])
                if pass1:
                    continue
                actf(hbf[:, 0:32, :], bt[:, 0:32, :], AF.Copy, [bk], ["hbf"])
                actf(hbf[:, 32:64, :], bt[:, 32:64, :], AF.Copy, [bk], ["hbf"], scale=-1.0)
                for Tt in range(8):
                    n = 0
                    for r_ in range(2):
                        for qq in range(4):
                            idx = r_ * 32 + 4 * Tt + qq
                            mm(ps[4][:, Tt * 16:(Tt + 1) * 16], cpd[:, idx, :], hbf[:, idx, :], n == 0, n == 7,
                               ["cpd", "hbf"], ["ps4"])
                            n += 1
                actf(yst[:, :, t0:t0 + 16], ps[4][:, 0:128].rearrange("p (k t) -> p k t", k=8), AF.Copy, ["ps4"], ["yst"])

        vmemset(Sst, 0.0, ["Sst"])
        for b in range(8):
            c0 = b * 128
            rmsnorm_block(c0, 128, lambda kt: pv[:, PV_GMIX + kt:PV_GMIX + kt + 1],
                          lambda kt: hblk[:, kt, :], ["hblk"], ["xT", f"x{b}"])
            for i in range(8):
                oc, kc = proj(16 + i)
                actf(cuT[:, i, :], oc, AF.Copy, [kc], ["cuT"])
            ssm_block(False, True)
            if b == 7:
                for i in range(4):
                    og, kg = proj(4 + i)
                    actf(tA[:, 0:128], og, AF.Sigmoid, [kg], ["tA"])
                    ov, kv = proj(i)
                    tt(zbuf[:, i, 30:158], ov, tA[:, 0:128], ALU.mult, [kv, "tA"], ["zbuf"])
        ts(sendb[:, 0:64], Sst, cmask[:, 0:1], None, ALU.mult, None, ["Sst", "cmask"], ["sendb"])
        ts(sendb[:, 64:184].rearrange("p (k t) -> p k t", k=4), zbuf[:, :, 128:158], cmask[:, 0:1], None, ALU.mult, None,
           ["zbuf", "cmask"], ["sendb"])
        dma(G_, bin_d[l], sendb[:], ["sendb"], [f"bin{l}"])
        if not USE_CC:
            dma(G_, bout_d[l], bin_d[l], [f"bin{l}"], [f"bout{l}"])
        else:
          S.add(G_, lambda e, l=l: e.collective_compute("AllReduce", ALU.add, replica_groups=[[0, 1], [2, 3], [4, 5], [6, 7]],
                                                      ins=[bin_d[l].opt()], outs=[bout_d[l].opt()]),
                reads=[f"bin{l}"], writes=[f"bout{l}"], cc=True)
        dma(SY, recvb[:], bout_d[l], [f"bout{l}"], ["recvb"])
        ts(Sst, recvb[:, 0:64], cmask[:, 1:2], None, ALU.mult, None, ["recvb", "cmask"], ["Sst"])
        ts(zbuf[:, :, 0:30], recvb[:, 64:184].rearrange("p (k t) -> p k t", k=4), cmask[:, 1:2], None, ALU.mult, None,
           ["recvb", "cmask"], ["zbuf"])

        pending = []
        for b in range(NB):
            smp = (b == 8)
            c0 = b * 128
            if smp:
                dma(SY, bsb, bs_d[l, :, 512:1024], (), ["bsb"])
            xk = [f"x{b}"] if l > 0 or True else []
            rmsnorm_block(c0, 128, lambda kt: pv[:, PV_GMIX + kt:PV_GMIX + kt + 1],
                          lambda kt: hblk[:, kt, :], ["hblk"], ["xT", f"x{b}"])

            for i in range(8):
                oc, kc = proj(16 + i)
                actf(cuT[:, i, :], oc, AF.Copy, [kc], ["cuT"])

            def a_thunk(i, smp=smp):
                def f():
                    og, kg = proj(4 + i)
                    actf(tA[:, 0:128], og, AF.Sigmoid, [kg], ["tA"])
                    ov, kv = proj(i)
                    if smp:
                        tt(zs[:, i, :, 30:38], ov.rearrange("p (s t) -> p s t", s=16),
                           tA[:, 0:128].rearrange("p (s t) -> p s t", s=16), ALU.mult, [kv, "tA"], ["zs"])
                    else:
                        tt(zbuf[:, i, 30:158], ov, tA[:, 0:128], ALU.mult, [kv, "tA"], ["zbuf"])
                return f

            def u_thunk(i):
                def f():
                    ou, ku = proj(8 + i)
                    actf(ub[:, i, :], ou, AF.Copy, [ku], ["ub"])
                return f

            def v_thunk(i):
                def f():
                    proj(12 + i, swap=True, out_ap=ps[6][:, i * 128:(i + 1) * 128])
                return f

            thunks = pending + [a_thunk(i) for i in range(4)] + [u_thunk(i) for i in range(4)] + [v_thunk(i) for i in range(4)]
            pending = []

            ssm_block(smp, False, thunks)
            if l == 0 and b in (0, 8):
                tap(f"hblk{b}", hblk[:], ["hblk"])
                tap(f"cuT{b}", cuT[:], ["cuT"])
                tap(f"ub{b}", ub[:], ["ub"])
                tap(f"z{b}", zs[:] if smp else zbuf[:], ["zs" if smp else "zbuf"])
                if b == 0:
                    tap("bbp", bbp[:], ["bbp"]); tap("AR", AR, ["AR"]); tap("AI2", AI2, ["AI2"]); tap("wsb", wsb, ["wsb"])
            if smp:
                dma(SY, ssms_o[l], Sso[:].rearrange("p k s -> p (k s)"), ["Sso"], [])
            if b == 7:
                dma(SY, ssmp_o[l], Sst, ["Sst"], [])
            for Tt in range(8):
                stt(yst[:, Tt, :], cuT[:, Tt, :], pv[:, PV_DSK + Tt:PV_DSK + Tt + 1], yst[:, Tt, :], ALU.mult, ALU.add,
                    ["cuT", "pv", "yst"], ["yst"])
            yf = yst[:].rearrange("p k t -> p (k t)")
            tt(tB, yf, yf, ALU.mult, ["yst"], ["tB"])
            ts(tB, tB, 0.044715 * 1.5957691216, 1.5957691216, ALU.mult, ALU.add, ["tB"], ["tB"])
            tt(tB, tB, yf, ALU.mult, ["tB", "yst"], ["tB"])
            actf(tB, tB, AF.Sigmoid, ["tB"], ["tB"])
            tt(zg[:].rearrange("p k t -> p (k t)"), yf, tB, ALU.mult, ["yst", "tB"], ["zg"])
            for Tt in range(8):
                mm(ps[5][:, 0:128], wgb[:, Tt, :], zg[:, Tt, :], True, True, ["wgb", "zg"], ["ps5"])
                mm(ps[5][:, 128:256], wgb[:, 8 + Tt, :], zg[:, Tt, :], True, True, ["wgb", "zg"], ["ps5"])
                actf(tC[:, 0:128], ps[5][:, 128:256], AF.Sigmoid, ["ps5", "pv"], ["tC"],
                     bias=pv[:, PV_BG + Tt:PV_BG + Tt + 1])
                stt(mix[:, 8 + Tt, :], ps[5][:, 0:128], pv[:, PV_BV + Tt:PV_BV + Tt + 1], tC[:, 0:128], ALU.add, ALU.mult,
                    ["ps5", "pv", "tC"], ["mix"])

            for i in range(4):
                acc = tA[:, 0:128]
                acc_v = acc.rearrange("p (s t) -> p s t", s=16) if smp else acc
                for k in range(31):
                    src = zs[:, i, :, k:k + 8] if smp else zbuf[:, i, k:k + 128]
                    wk = pv[:, PV_CW + i * 31 + k:PV_CW + i * 31 + k + 1]
                    if k == 0:
                        ts(acc_v, src, wk, None, ALU.mult, None, ["zs" if smp else "zbuf", "pv"], ["tA"])
                    else:
                        stt(acc_v, src, wk, acc_v, ALU.mult, ALU.add, ["zs" if smp else "zbuf", "pv", "tA"], ["tA"], ses=False)
                ts(acc, acc, pv[:, PV_CB + i:PV_CB + i + 1], None, ALU.add, None, ["tA", "pv"], ["tA"])
                sq = tA[:, 128:256]
                tt(sq, acc, acc, ALU.mult, ["tA"], ["tA2"])
                mm(ps[5][:, 0:128], ones_f[:], acc, True, True, ["ones_f", "tA"], ["ps5"])
                mm(ps[5][:, 128:256], ones_f[:], sq, True, True, ["ones_f", "tA2"], ["ps5"])
                m2 = tA[:, 256:384]; var = tA[:, 384:512]
                msb = tC[:, 128:256]
                actf(msb, ps[5][:, 0:128], AF.Copy, ["ps5"], ["tCm"])
                tt(m2, msb, msb, ALU.mult, ["tCm"], ["tA3"])
                tt(var, ps[5][:, 128:256], m2, ALU.subtract, ["ps5", "tA3"], ["tA4"])
                rsqrt(var, var, 1.0, ["tA4"], ["tA4"])
                tt(m2, acc, msb, ALU.subtract, ["tA", "tCm"], ["tA3"])
                tt(m2, m2, var, ALU.mult, ["tA3", "tA4"], ["tA3"])
                actf(mix[:, i, :], m2, AF.Silu, ["tA3", "pv"], ["mix"],
                     scale=pv[:, PV_LAG + i:PV_LAG + i + 1], bias=pv[:, PV_LAB + i:PV_LAB + i + 1])
            if smp:
                dma(SY, convs_o[l].rearrange("p (k s t) -> p k s t", k=4, s=16), zs[:, :, :, 8:38], ["zs"], [])
            else:
                if b == 7:
                    dma(SY, convp_o[l].rearrange("p (k t) -> p k t", k=4), zbuf[:, :, 128:158], ["zbuf"], [])
                vcopy(tC[:, 0:120].rearrange("p (k t) -> p k t", k=4), zbuf[:, :, 128:158], ["zbuf"], ["tC"])
                vcopy(zbuf[:, :, 0:30], tC[:, 0:120].rearrange("p (k t) -> p k t", k=4), ["tC"], ["zbuf"])

            s1, s2 = small[:, 0:1], small[:, 1:2]
            actf(junk, ps[6][:], AF.Copy, ["ps6"], ["vnf", "small"], accum=s1)
            actf(junk, ps[6][:], AF.Square, ["ps6"], ["vnf", "small"], accum=s2)
            mean, msq, varv = small[:, 2:3], small[:, 3:4], small[:, 4:5]
            ts(mean, s1, 1.0 / 512, None, ALU.mult, None, ["small"], ["small"])
            tt(msq, mean, mean, ALU.mult, ["small"], ["small"])
            stt(varv, s2, 1.0 / 512, msq, ALU.mult, ALU.subtract, ["small"], ["small"])
            rsqrt(varv, varv, 1.0, ["small"], ["small"])
            ts(vnf, ps[6][:], mean, None, ALU.subtract, None, ["ps6", "small"], ["vnf"])
            ts(vnf, vnf, varv, None, ALU.mult, None, ["vnf", "small"], ["vnf"])
            tt(vnf, vnf, lnv[:, 0:512], ALU.mult, ["vnf", "lnv"], ["vnf"])
            tt(vnf, vnf, lnv[:, 512:1024], ALU.add, ["vnf", "lnv"], ["vnf"])
            actf(vnb, vnf, AF.Copy, ["vnf"], ["vnb"])
            if smp:
                dma(SY, vs_o[l], vnf, ["vnf"], [])
            wo = 512 if smp else 0
            for h in range(4):
                mm(ps[5][:, h * 128:(h + 1) * 128], vnb[:, h * 128:(h + 1) * 128], wsb[:, wo + h * 128:wo + (h + 1) * 128],
                   True, True, ["vnb", "wsb"], ["ps5"])
            for h in range(4):
                tt(tC[:, 0:128], ps[5][:, h * 128:(h + 1) * 128], bsb[:, h * 128:(h + 1) * 128], ALU.add,
                   ["ps5", "bsb"], ["tC"])
                tt(mix[:, 4 + h, :], tC[:, 0:128], ub[:, h, :], ALU.mult, ["tC", "ub"], ["mix"])

            if l == 0 and b in (0, 8):
                tap(f"yst{b}", yst[:], ["yst"])
                tap(f"vnf{b}", vnf, ["vnf"])
                tap(f"mix{b}", mix[:], ["mix"])
            def o_thunk(ft, c0=c0, b=b):
                def f():
                    wi = load_w(w_out_d[l, ft])
                    o, ok = col_slot()
                    for kt in range(16):
                        mm(o, win[wi][:, kt, :], mix[:, kt, :], kt == 0, kt == 15, ["mix", f"win{wi}"], [ok])
                    tt(xT[:, ft, c0:c0 + 128], xT[:, ft, c0:c0 + 128], o, ALU.add, [ok, f"x{b}", "xT"], [f"x{b}"])
                return f
            pending = [o_thunk(ft) for ft in range(16)]
        for th in pending:
            th()
        pending = []

        if l == 0:
            tap("xmid", xT[:, :, 0:128], ["xT", "x0"])
            tap("xmid8", xT[:, :, 1024:1152], ["xT", "x8"])
        allk = ["hblk", "mix", "xsq", "cuT", "vnb", "hbf", "zg", "wsb", "wgb", "bbp", "cpd", "zbuf", "zs", "ub", "vnf", "junk",
                "bu", "bu0", "bu1", "yst", "tA", "tA2", "tA3", "tA4", "tB", "tBa", "tBb", "tC", "tCa", "tCb", "tCm", "lnv", "wsf", "bsb", "Sst", "Ssm",
                "Sso", "AR", "AI2", "AR4", "AI4", "rt1", "rt2", "rt1b", "rt2b", "PW", "A16", "Evec", "sS", "g0", "g1", "g2", "g3", "g4", "gin0", "gin1", "gin2"]
        ffk = ["hT", "act", "wgu0", "wgu1", "wdn0", "wdn1", "sgt"]
        for e in ENGS:
            S.add(e, lambda eng: eng.nop(), reads=(), writes=allk + ffk)
        for b in range(NB):
            c0 = b * 128
            rmsnorm_block(c0, 128, lambda kt: pv[:, PV_GFFN + kt:PV_GFFN + kt + 1],
                          lambda kt, c0=c0: hT_all[:, kt, c0:c0 + 128], ["hT"], ["xT", f"x{b}"])
        TBS = [(0, 512), (512, 512), (1024, 128)]
        gi = 0; di = 0
        for fb in range(4):
            for j in range(11):
                ff = fb * 11 + j
                gs = gi % 2; gi += 1
                dma(G_, wgu[gs][:].rearrange("p k c -> p (k c)"), w_gu_d[l, ff], (), [f"wgu{gs}"])
                for (t0, n) in TBS:
                    pa = next_ps(); pb = next_ps()
                    for kt in range(16):
                        mm(ps[pa][:, :n], wgu[gs][:, kt, 0:128], hT_all[:, kt, t0:t0 + n], kt == 0, kt == 15,
                           ["hT", f"wgu{gs}"], [f"ps{pa}"])
                    for kt in range(16):
                        mm(ps[pb][:, :n], wgu[gs][:, kt, 128:256], hT_all[:, kt, t0:t0 + n], kt == 0, kt == 15,
                           ["hT", f"wgu{gs}"], [f"ps{pb}"])
                    actf(sgt[:, :n], ps[pa][:, :n], AF.Silu, [f"ps{pa}"], ["sgt"])
                    tt(act[:, j, t0:t0 + n], sgt[:, :n], ps[pb][:, :n], ALU.mult, ["sgt", f"ps{pb}"], ["act"])
            for ft in range(16):
                ds_ = di % 2; di += 1
                dma(G_, wdn[ds_][:].rearrange("p k c -> p (k c)"), w_dn_d[l, fb * 16 + ft], (), [f"wdn{ds_}"])
                for ti, (t0, n) in enumerate(TBS):
                    p = next_ps()
                    for j in range(11):
                        mm(ps[p][:, :n], wdn[ds_][:, j, :], act[:, j, t0:t0 + n], j == 0, j == 10, ["act", f"wdn{ds_}"], [f"ps{p}"])
                    xks = [f"x{bb}" for bb in range(t0 // 128, (t0 + n) // 128)]
                    tt(xT[:, ft, t0:t0 + n], xT[:, ft, t0:t0 + n], ps[p][:, :n], ALU.add, [f"ps{p}", "xT"] + xks, xks)
        for e in ENGS:
            S.add(e, lambda eng: eng.nop(), reads=(), writes=allk + ffk)

    tap("xend", xT[:, :, 0:128], ["xT", "x0"])
    yv = yT_o.rearrange("p (k t) -> p k t", k=16)
    for b in range(NB):
        c0 = b * 128
        ob = R[:, 8192 + (b % 2) * 4096:8192 + (b % 2) * 4096 + 4096].bitcast(F32).rearrange("p (k t) -> p k t", k=16)
        rmsnorm_block(c0, 128, lambda kt: gfin[:, kt:kt + 1], lambda kt, ob=ob: ob[:, kt, :], [f"ob{b % 2}"], ["xT", f"x{b}"])
        dma(SY, yv[:, :, c0:c0 + 128], ob, [f"ob{b % 2}"], [])

    with nc.Block() as block:
        S.emit(nc, block, st)
    st.close()
    return nc


_NC = None


def _prep(inputs):
    f = lambda a: np.ascontiguousarray(np.asarray(a, dtype=np.float32))
    x_prompt = f(inputs["x_prompt"]); x_sample = f(inputs["x_sample"])
    sc = f(inputs["state_conv"]); sre = f(inputs["state_ssm_re"]); sim = f(inputs["state_ssm_im"])
    L = DEPTH
    w_in = f(inputs["w_in"]).reshape(L, 16, 128, 24, 128).transpose(0, 3, 2, 1, 4).reshape(L, 24, 128, 2048)
    w_out = f(inputs["w_out"]).reshape(L, 16, 128, 16, 128).transpose(0, 3, 2, 1, 4).reshape(L, 16, 128, 2048)
    wgu = f(inputs["w_gu"]).reshape(L, 16, 128, 2, 44, 128).transpose(0, 4, 2, 1, 3, 5).reshape(L, 44, 128, 4096)
    wdn = f(inputs["w_down"]).reshape(L, 4, 11, 128, 16, 128).transpose(0, 1, 4, 3, 2, 5).reshape(L, 64, 128, 1408)
    w_in = np.ascontiguousarray(w_in); w_out = np.ascontiguousarray(w_out)
    wgu = np.ascontiguousarray(wgu); wdn = np.ascontiguousarray(wdn)
    pvec = np.zeros((L, 128, 256), np.float32)
    col = lambda v, k: v.reshape(k, 128).T
    lnv = np.zeros((L, 128, 1024), np.float32); ws = np.zeros((L, 128, 1024), np.float32)
    bs = np.zeros((L, 128, 1024), np.float32)
    ssmR = np.zeros((L, 3, 128, 4096), np.float32); bpad = np.zeros((L, 2, 128, 4096), np.float32)
    cpad = np.zeros((L, 128, 64, 128), np.float32); ssmS = np.zeros((L, 128, 768), np.float32)
    wg = np.zeros((L, 128, 16, 128), np.float32)
    for l in range(L):
        pvec[l, :, 0:16] = col(f(inputs["g_mix"])[l], 16)
        pvec[l, :, 16:32] = col(f(inputs["g_ffn"])[l], 16)
        pvec[l, :, 32:156] = f(inputs["conv_w"])[l].reshape(31, 4, 128).transpose(2, 1, 0).reshape(128, 124)
        pvec[l, :, 156:160] = col(f(inputs["conv_b"])[l], 4)
        pvec[l, :, 160:164] = col(f(inputs["ln_a_g"])[l], 4)
        pvec[l, :, 164:168] = col(f(inputs["ln_a_b"])[l], 4)
        pvec[l, :, 168:176] = f(inputs["d_skip"])[l].reshape(8, 128).T
        bgl = f(inputs["b_glu"])[l]
        pvec[l, :, 176:184] = bgl[:, :16].reshape(8, 128).T
        pvec[l, :, 184:192] = bgl[:, 16:].reshape(8, 128).T
        lnv[l, :, 0:512] = f(inputs["ln_v_g"])[l][None, :]
        lnv[l, :, 512:1024] = f(inputs["ln_v_b"])[l][None, :]
        wsl = f(inputs["w_s"])[l]
        ws[l, :, 0:512] = wsl.transpose(2, 0, 1).reshape(128, 512)
        blk = np.zeros((128, 4, 128), np.float32)
        for qd in range(16):
            blk[8 * qd:8 * qd + 8, :, 8 * qd:8 * qd + 8] = wsl[:, 0:8, 0:8].transpose(2, 0, 1)
        ws[l, :, 512:1024] = blk.reshape(128, 512)
        bsl = f(inputs["b_s"])[l]
        bs[l, :, 0:512] = bsl.reshape(1, 512)
        bs[l, :, 512:1024] = np.tile(bsl[:, 0:8], (1, 16)).reshape(1, 512)
        are = f(inputs["a_re"])[l]; aim = f(inputs["a_im"])[l]; ldt = f(inputs["log_dt"])[l]
        ssmR[l, 0] = are.reshape(1, 4096); ssmR[l, 1] = aim.reshape(1, 4096)
        ssmR[l, 2] = np.repeat(ldt, 64).reshape(1, 4096)
        ssmS[l, :, 0:32] = are.reshape(32, 2, 64).transpose(1, 2, 0).reshape(128, 32)
        ssmS[l, :, 256:288] = aim.reshape(32, 2, 64).transpose(1, 2, 0).reshape(128, 32)
        ssmS[l, :, 512:544] = np.broadcast_to(ldt.reshape(32, 2).T[:, None, :], (2, 64, 32)).reshape(128, 32)
        bre = f(inputs["b_re"])[l]; bim = f(inputs["b_im"])[l]
        cre = f(inputs["c_re"])[l]; cim = f(inputs["c_im"])[l]
        wgl = f(inputs["w_glu"])[l]
        bp = np.zeros((2, 128, 32, 2, 64), np.float32)
        cp = np.zeros((2, 64, 64, 128), np.float32)
        for g in range(64):
            q, g2 = divmod(g, 2); g8 = g % 8; T = g // 8
            bp[0, 16 * g8:16 * g8 + 16, q, g2, :] = bre[g].T
            bp[1, 16 * g8:16 * g8 + 16, q, g2, :] = bim[g].T
            cp[g2, :, q, 16 * g8:16 * g8 + 16] = cre[g].T
            cp[g2, :, 32 + q, 16 * g8:16 * g8 + 16] = cim[g].T
            wg[l, 16 * g8:16 * g8 + 16, T, 16 * g8:16 * g8 + 16] = wgl[g][:, :16]
            wg[l, 16 * g8:16 * g8 + 16, 8 + T, 16 * g8:16 * g8 + 16] = wgl[g][:, 16:]
        bpad[l] = bp.reshape(2, 128, 4096)
        cpad[l] = cp.reshape(128, 64, 128)
    mask = np.tile(np.triu(np.ones((128, 128), np.float32))[:, None, :], (1, 4, 1)).reshape(128, 512)
    gfin = col(f(inputs["g_final"]), 16)
    shared = dict(w_in=w_in, w_out=w_out, w_gu=wgu, w_dn=wdn, pvec=pvec, gfin=np.ascontiguousarray(gfin),
                  lnv=lnv, ws=ws, bs=bs, mask=mask, ssmR=ssmR, bpad=bpad, cpad=cpad.reshape(L, 128, 8192), ssmS=ssmS,
                  wg=wg.reshape(L, 128, 2048))
    in_maps = []
    for c in range(NCORES):
        pr, half = divmod(c, 2)
        xin = np.concatenate([x_prompt[pr, half * LT:(half + 1) * LT], x_sample[16 * c:16 * c + 16].reshape(128, D)], axis=0)
        xT = np.ascontiguousarray(xin.T.reshape(16, 128, NT).transpose(1, 0, 2)).reshape(128, 16 * NT)
        zs0 = np.ascontiguousarray(sc[:, 16 * c:16 * c + 16].reshape(L, 16, 30, 4, 128).transpose(0, 4, 3, 1, 2)).reshape(L, 128, 1920)
        def sl(a):
            return a[:, 16 * c:16 * c + 16].reshape(L, 16, 32, 2, 64).transpose(0, 3, 4, 2, 1).reshape(L, 128, 32, 16)
        ss0 = np.ascontiguousarray(np.stack([sl(sre), sl(sim)], axis=2)).reshape(L, 128, 1024)
        cm = np.zeros((128, 2), np.float32); cm[:, 0] = 1.0 - half; cm[:, 1] = float(half)
        m = dict(shared); m.update(xT=xT, zs0=zs0, ss0=ss0, cmask=cm)
        in_maps.append(m)
    return in_maps


def kernel(**inputs):
    global _NC
    if _NC is None:
        _NC = build_program()
    in_maps = _prep(inputs)
    res = run_bass_kernel_spmd(_NC, in_maps, core_ids=list(range(NCORES)))
    R = res.results
    L = DEPTH
    y_prompt = np.zeros((4, 2048, D), np.float32); y_sample = np.zeros((128, 8, D), np.float32)
    conv_p = np.zeros((L, 4, 30, 512), np.float32)
    ssr_p = np.zeros((L, 4, 64, 64), np.float32); ssi_p = np.zeros((L, 4, 64, 64), np.float32)
    conv_s = np.zeros((L, 128, 30, 512), np.float32)
    ssr_s = np.zeros((L, 128, 64, 64), np.float32); ssi_s = np.zeros((L, 128, 64, 64), np.float32)
    v_s = np.zeros((L, 128, 8, 512), np.float32)
    for c in range(NCORES):
        pr, half = divmod(c, 2)
        y = np.asarray(R[c]["yT"]).reshape(128, 16, NT).transpose(2, 1, 0).reshape(NT, D)
        y_prompt[pr, half * LT:(half + 1) * LT] = y[:LT]
        y_sample[16 * c:16 * c + 16] = y[LT:].reshape(16, 8, D)
        cs = np.asarray(R[c]["conv_s"]).reshape(L, 128, 4, 16, 30).transpose(0, 3, 4, 2, 1).reshape(L, 16, 30, 512)
        conv_s[:, 16 * c:16 * c + 16] = cs
        ss = np.asarray(R[c]["ssm_s"]).reshape(L, 2, 64, 2, 32, 16)
        ss = ss.transpose(0, 3, 5, 4, 1, 2).reshape(L, 2, 16, 64, 64)
        ssr_s[:, 16 * c:16 * c + 16] = ss[:, 0]; ssi_s[:, 16 * c:16 * c + 16] = ss[:, 1]
        v_s[:, 16 * c:16 * c + 16] = np.asarray(R[c]["v_s"]).reshape(L, 16, 8, 512)
        if half == 1:
            conv_p[:, pr] = np.asarray(R[c]["conv_p"]).reshape(L, 128, 4, 30).transpose(0, 3, 2, 1).reshape(L, 30, 512)
            sp = np.asarray(R[c]["ssm_p"]).reshape(L, 2, 64, 2, 32).transpose(0, 3, 4, 1, 2).reshape(L, 2, 64, 64)
            ssr_p[:, pr] = sp[:, 0]; ssi_p[:, pr] = sp[:, 1]
    return (y_prompt, y_sample, conv_p, ssr_p, ssi_p, conv_s, ssr_s, ssi_s, v_s)
```
